# Optimizing a Trainium2 kernel written in Bass

```python
import math
import jax, jax.numpy as jnp
from jax import lax
import numpy as np

D_MODEL = 1024
BATCH = 8
SEQ = 2048
DEPTH = 2

CHUNK = 64
Q_BLOCK = 128
ROPE_THETA = 500000.0
N_MIXERS = 2
N_A_LAYERS = (DEPTH + 1) // 2
N_B_LAYERS = DEPTH // 2

A_HEADS = 8
A_HEAD_DIM = 64
A_ROT = A_HEAD_DIM // 4
A_QK = A_HEADS * 2 * A_HEAD_DIM
A_V = A_HEADS * 2 * A_HEAD_DIM
A_WIDTH = A_V
A_IN = 2 * A_QK + A_V + A_WIDTH

B_HEADS = 16
B_NOPE = 64
B_ROPE = 32
B_VDIM = 64
B_Q_RANK = 512
B_KV_RANK = 256
B_WIDTH = B_HEADS * B_VDIM
B_IN = B_Q_RANK + B_KV_RANK + B_ROPE + B_WIDTH

DEEPNORM_ALPHA = (2.0 * DEPTH) ** 0.25
DEEPNORM_BETA = (8.0 * DEPTH) ** -0.25
LN_EPS = 1e-5
RMS_EPS = 1e-6
SUBLN_EPS = 1e-5
NEG_INF = -1e30
POS_OFFSET_MAX_CHUNKS = 64

kernel_name = "hybrid_diffattn_mla_deepnorm_adaln"


def _layer_norm(x, g, b):
    xf = x.astype(jnp.float32)
    mu = jnp.mean(xf, axis=-1, keepdims=True)
    var = jnp.mean(jnp.square(xf - mu), axis=-1, keepdims=True)
    return ((xf - mu) * lax.rsqrt(var + LN_EPS) * g.astype(jnp.float32)
            + b.astype(jnp.float32)).astype(x.dtype)


def _rms_norm(x, g, eps):
    xf = x.astype(jnp.float32)
    ms = jnp.mean(jnp.square(xf), axis=-1, keepdims=True)
    return (xf * lax.rsqrt(ms + eps) * g.astype(jnp.float32)).astype(x.dtype)


def _rope_cos_sin(positions, rot_dim):
    inv_freq = ROPE_THETA ** (-jnp.arange(0, rot_dim, 2, dtype=jnp.float32) / rot_dim)
    ang = positions.astype(jnp.float32)[..., None] * inv_freq
    return jnp.cos(ang), jnp.sin(ang)


def _rotate(x, cos, sin):
    half = x.shape[-1] // 2
    x1 = x[..., :half].astype(jnp.float32)
    x2 = x[..., half:].astype(jnp.float32)
    return jnp.concatenate([x1 * cos - x2 * sin, x2 * cos + x1 * sin], axis=-1).astype(x.dtype)


def _chunk_mask(q_start, seq):
    q_chunk = (q_start + jnp.arange(Q_BLOCK)) // CHUNK
    k_chunk = jnp.arange(seq) // CHUNK
    return k_chunk[None, :] <= q_chunk[:, None]


def _sweep_query_blocks(block_fn, seq):
    n_blocks = seq // Q_BLOCK
    out = lax.map(block_fn, jnp.arange(n_blocks) * Q_BLOCK)
    out = jnp.moveaxis(out, 0, 1)
    return out.reshape(out.shape[0], n_blocks * Q_BLOCK, *out.shape[3:])


def _diff_attention_branch(u, cos, sin, w_in, lq1, lk1, lq2, lk2, subln_g, w_out, lambda_init):
    b, s, _ = u.shape
    proj = u @ w_in
    q, k, v, gate = jnp.split(proj, [A_QK, 2 * A_QK, 2 * A_QK + A_V], axis=-1)
    q = q.reshape(b, s, A_HEADS, 2, A_HEAD_DIM)
    k = k.reshape(b, s, A_HEADS, 2, A_HEAD_DIM)
    v = v.reshape(b, s, A_HEADS, 2 * A_HEAD_DIM)
    cs, sn = cos[:, :, None, None, :], sin[:, :, None, None, :]
    q = jnp.concatenate([_rotate(q[..., :A_ROT], cs, sn), q[..., A_ROT:]], axis=-1)
    k = jnp.concatenate([_rotate(k[..., :A_ROT], cs, sn), k[..., A_ROT:]], axis=-1)
    f32 = jnp.float32
    lam = (jnp.exp(jnp.sum(lq1.astype(f32) * lk1.astype(f32)))
           - jnp.exp(jnp.sum(lq2.astype(f32) * lk2.astype(f32))) + lambda_init)
    scale = A_HEAD_DIM ** -0.5

    def block(q_start):
        qb = lax.dynamic_slice_in_dim(q, q_start, Q_BLOCK, axis=1)
        sc = jnp.einsum('bqhmd,bkhmd->bhmqk', qb, k, preferred_element_type=f32) * scale
        p = jax.nn.softmax(jnp.where(_chunk_mask(q_start, s), sc, NEG_INF), axis=-1)
        attn = p[:, :, 0] - lam * p[:, :, 1]
        return jnp.einsum('bhqk,bkhe->bqhe', attn.astype(v.dtype), v)

    o = _sweep_query_blocks(block, s)
    o = _rms_norm(o, subln_g, SUBLN_EPS) * (1.0 - lambda_init)
    o = o.reshape(b, s, A_WIDTH) * jax.nn.silu(gate)
    return o @ w_out


def _mla_branch(u, cos, sin, w_in, q_norm_g, w_uq, kv_norm_g, w_ukv, w_out):
    b, s, _ = u.shape
    proj = u @ w_in
    q_lat, kv_lat, k_rope, gate = jnp.split(
        proj, [B_Q_RANK, B_Q_RANK + B_KV_RANK, B_Q_RANK + B_KV_RANK + B_ROPE], axis=-1)
    q = (_rms_norm(q_lat, q_norm_g, RMS_EPS) @ w_uq).reshape(b, s, B_HEADS, B_NOPE + B_ROPE)
    q_nope = q[..., :B_NOPE]
    q_rope = _rotate(q[..., B_NOPE:], cos[:, :, None], sin[:, :, None])
    kv = (_rms_norm(kv_lat, kv_norm_g, RMS_EPS) @ w_ukv).reshape(b, s, B_HEADS, B_NOPE + B_VDIM)
    k_nope, v = kv[..., :B_NOPE], kv[..., B_NOPE:]
    k_rope = _rotate(k_rope, cos, sin)
    f32 = jnp.float32
    scale = (B_NOPE + B_ROPE) ** -0.5

    def block(q_start):
        qn = lax.dynamic_slice_in_dim(q_nope, q_start, Q_BLOCK, axis=1)
        qr = lax.dynamic_slice_in_dim(q_rope, q_start, Q_BLOCK, axis=1)
        sc = (jnp.einsum('bqhd,bkhd->bhqk', qn, k_nope, preferred_element_type=f32)
              + jnp.einsum('bqhr,bkr->bhqk', qr, k_rope, preferred_element_type=f32)) * scale
        p = jax.nn.softmax(jnp.where(_chunk_mask(q_start, s), sc, NEG_INF), axis=-1)
        return jnp.einsum('bhqk,bkhd->bqhd', p.astype(v.dtype), v)

    o = _sweep_query_blocks(block, s)
    o = o.reshape(b, s, B_WIDTH) * jax.nn.silu(gate)
    return o @ w_out


def setup_inputs(seed: int = 0) -> dict:
    key = jax.random.key(seed)
    ks = jax.random.split(key, 24)
    nrm = lambda k, shape, sc: jax.random.normal(k, shape, jnp.float32) * sc
    x = nrm(ks[0], (BATCH, SEQ, D_MODEL), 1.0)
    c = nrm(ks[1], (BATCH, D_MODEL), 1.0)
    start = jax.random.randint(ks[2], (BATCH, 1), 0, POS_OFFSET_MAX_CHUNKS, dtype=jnp.int32) * CHUNK
    positions = (start + jnp.arange(SEQ, dtype=jnp.int32)[None, :]).astype(jnp.int32)
    return {
        "x": x,
        "c": c,
        "positions": positions,
        "ada_w": nrm(ks[3], (DEPTH, D_MODEL, 3 * D_MODEL), 0.5 * D_MODEL ** -0.5),
        "ada_b": nrm(ks[4], (DEPTH, 3 * D_MODEL), 0.02),
        "ln_g": 1.0 + nrm(ks[5], (DEPTH, D_MODEL), 0.02),
        "ln_b": nrm(ks[6], (DEPTH, D_MODEL), 0.02),
        "a_w_in": nrm(ks[7], (N_A_LAYERS, D_MODEL, A_IN), D_MODEL ** -0.5),
        "a_lambda_q1": nrm(ks[8], (N_A_LAYERS, A_HEAD_DIM), 0.1),
        "a_lambda_k1": nrm(ks[9], (N_A_LAYERS, A_HEAD_DIM), 0.1),
        "a_lambda_q2": nrm(ks[10], (N_A_LAYERS, A_HEAD_DIM), 0.1),
        "a_lambda_k2": nrm(ks[11], (N_A_LAYERS, A_HEAD_DIM), 0.1),
        "a_subln_g": 1.0 + nrm(ks[12], (N_A_LAYERS, 2 * A_HEAD_DIM), 0.02),
        "a_w_out": nrm(ks[13], (N_A_LAYERS, A_WIDTH, D_MODEL), DEEPNORM_BETA * A_WIDTH ** -0.5),
        "b_w_in": nrm(ks[14], (N_B_LAYERS, D_MODEL, B_IN), D_MODEL ** -0.5),
        "b_q_norm_g": 1.0 + nrm(ks[15], (N_B_LAYERS, B_Q_RANK), 0.02),
        "b_w_uq": nrm(ks[16], (N_B_LAYERS, B_Q_RANK, B_HEADS * (B_NOPE + B_ROPE)), B_Q_RANK ** -0.5),
        "b_kv_norm_g": 1.0 + nrm(ks[17], (N_B_LAYERS, B_KV_RANK), 0.02),
        "b_w_ukv": nrm(ks[18], (N_B_LAYERS, B_KV_RANK, B_HEADS * (B_NOPE + B_VDIM)), B_KV_RANK ** -0.5),
        "b_w_out": nrm(ks[19], (N_B_LAYERS, B_WIDTH, D_MODEL), DEEPNORM_BETA * B_WIDTH ** -0.5),
    }


def reference(x, c, positions, ada_w, ada_b, ln_g, ln_b,
              a_w_in, a_lambda_q1, a_lambda_k1, a_lambda_q2, a_lambda_k2, a_subln_g, a_w_out,
              b_w_in, b_q_norm_g, b_w_uq, b_kv_norm_g, b_w_ukv, b_w_out):
    cos_a, sin_a = _rope_cos_sin(positions, A_ROT)
    cos_b, sin_b = _rope_cos_sin(positions, B_ROPE)
    c_act = jax.nn.silu(c)
    for i in range(DEPTH):
        shift, scale, gate = jnp.split(c_act @ ada_w[i] + ada_b[i], 3, axis=-1)
        u = x * (1.0 + scale[:, None, :]) + shift[:, None, :]
        j = i // N_MIXERS
        if i % N_MIXERS == 0:
            lambda_init = 0.8 - 0.6 * math.exp(-0.3 * i)
            y = _diff_attention_branch(u, cos_a, sin_a, a_w_in[j], a_lambda_q1[j], a_lambda_k1[j],
                                       a_lambda_q2[j], a_lambda_k2[j], a_subln_g[j], a_w_out[j],
                                       lambda_init)
        else:
            y = _mla_branch(u, cos_b, sin_b, b_w_in[j], b_q_norm_g[j], b_w_uq[j],
                            b_kv_norm_g[j], b_w_ukv[j], b_w_out[j])
        x = _layer_norm(DEEPNORM_ALPHA * x + gate[:, None, :] * y, ln_g[i], ln_b[i])
    return x
```

```python
import math
import numpy as np
import concourse.bass as bass
import concourse.mybir as mybir
from concourse.bass_utils import run_bass_kernel_spmd

F32 = mybir.dt.float32
BF16 = mybir.dt.bfloat16
I32 = mybir.dt.int32
AF = mybir.ActivationFunctionType
ALU = mybir.AluOpType


COMPUTE = ("pe", "act", "dve", "pool")
DMAQ = ("sp", "poolq")


class Op:
    __slots__ = ("eng", "fn", "deps", "raw_same", "signals", "sig", "sem", "waits",
                 "clock", "is_dma", "idx", "tag", "pos", "really")

    def __init__(self, eng, fn, tag=""):
        self.eng = eng
        self.fn = fn
        self.deps = []
        self.signals = False
        self.sig = 0
        self.sem = None
        self.waits = []
        self.clock = None
        self.is_dma = eng in DMAQ
        self.tag = tag


class Sched:
    def __init__(self, nc):
        self.nc = nc
        self.ops = []
        self.last_w = {}
        self.readers = {}
        self.dma_sem_of = {}
        self.dma_cnt = {}

    def op(self, eng, fn, reads=(), writes=(), dma_key=None, tag=""):
        o = Op(eng, fn, tag)
        o.idx = len(self.ops)
        reads = list(reads)
        writes = list(writes)
        deps = {}
        for k in reads:
            w = self.last_w.get(k)
            if w is not None:
                deps[w.idx] = (w, True)
            if isinstance(k, tuple) and k and k[0] == "ps":
                for r in self.readers.get(k, ()):
                    if r.eng != eng and r.idx not in deps:
                        deps[r.idx] = (r, False)
        for k in writes:
            w = self.last_w.get(k)
            if w is not None and w.idx not in deps:
                deps[w.idx] = (w, False)
            for r in self.readers.get(k, ()):
                if r.idx not in deps:
                    deps[r.idx] = (r, False)
        for k in writes:
            self.last_w[k] = o
            self.readers[k] = []
        for k in reads:
            self.readers.setdefault(k, []).append(o)
        for (x, raw) in deps.values():
            if x is o:
                continue
            if x.eng == o.eng and not x.is_dma:
                if o.eng == "pe" or (not raw and o.eng != "pool"):
                    continue
            o.deps.append(x)
            x.signals = True
        if o.is_dma:
            key = dma_key if dma_key is not None else (writes[0] if writes else reads[0])
            sname = self.dma_sem_of.setdefault(key, "dq%d" % len(self.dma_sem_of))
            self.dma_cnt[sname] = self.dma_cnt.get(sname, 0) + 16
            o.sem = sname
            o.sig = self.dma_cnt[sname]
        self.ops.append(o)
        return o

    def barrier_keys(self, keys):
        pass

    def emit(self, final_wait_ops=()):
        nc = self.nc
        cnt = {e: 0 for e in COMPUTE}
        for o in self.ops:
            o.really = o.is_dma
            if o.is_dma:
                o.pos = o.sig
                continue
            o.sem = "e_" + o.eng
            cnt[o.eng] += 1
            o.pos = cnt[o.eng]
        know = {e: {} for e in COMPUTE + DMAQ}
        for o in self.ops:
            K = know[o.eng]
            need = {}
            for x in o.deps:
                if x.pos > need.get(x.sem, (0, None))[0]:
                    need[x.sem] = (x.pos, x)
            waits = [(s_, x) for s_, (v, x) in need.items() if K.get(s_, 0) < v]
            for x in o.deps:
                if x.clock:
                    for s_, v in x.clock.items():
                        if K.get(s_, 0) < v:
                            K[s_] = v
            for s_, x in waits:
                x.really = True
                if K.get(s_, 0) < x.pos:
                    K[s_] = x.pos
            o.waits = waits
            if o.signals or o.is_dma:
                c = dict(K)
                c[o.sem] = max(c.get(o.sem, 0), o.pos)
                o.clock = c
        cnt = {e: 0 for e in COMPUTE}
        for o in self.ops:
            if o.is_dma:
                continue
            if o.really:
                cnt[o.eng] += 1
            o.sig = cnt[o.eng]
            o.signals = o.really
        for o in self.ops:
            o.waits = [(s_, x.sig) for s_, x in o.waits]
        self.n_incs = sum(1 for o in self.ops if o.signals and not o.is_dma)
        names = ["e_" + e for e in COMPUTE] + sorted(set(self.dma_cnt))
        self.n_waits = sum(len(o.waits) for o in self.ops)
        sems = {}
        import contextlib
        with contextlib.ExitStack() as st:
            for n in names:
                sems[n] = st.enter_context(nc.semaphore(n))
            block = st.enter_context(nc.Block())
            per = {e: [o for o in self.ops if o.eng == e] for e in COMPUTE + DMAQ}

            def run(engobj, lst, extra_final=()):
                for o in lst:
                    for s, v in o.waits:
                        engobj.wait_ge(sems[s], v)
                    ins = o.fn(engobj)
                    if o.is_dma:
                        ins.then_inc(sems[o.sem], 16)
                    elif o.signals:
                        ins.then_inc(sems[o.sem], 1)
                for (s, v) in extra_final:
                    engobj.wait_ge(sems[s], v)

            fmax = {}
            for o in final_wait_ops:
                fmax[o.sem] = max(fmax.get(o.sem, 0), o.sig)
            finals = sorted(fmax.items())

            @block.tensor
            def _(e):
                run(e, per["pe"])

            @block.scalar
            def _(e):
                run(e, per["act"])

            @block.vector
            def _(e):
                run(e, per["dve"])

            @block.gpsimd
            def _(e):
                merged = sorted(per["pool"] + per["poolq"], key=lambda o: o.idx)
                run(e, merged)

            @block.sync
            def _(e):
                run(e, per["sp"], finals)


def _alias(self, new, olds):
    ops = []
    for k in olds:
        w = self.last_w.get(k)
        if w is not None:
            ops.append(w)
        ops += self.readers.get(k, [])
    for k in new:
        self.last_w.pop(k, None)
        self.readers[k] = list(ops)


Sched.alias = _alias


import os
Q2 = os.environ.get("KQ2", "sp")
S_LEN = 2048
D = 1024
NTB = 16
DEPTH = 2
ALPHA = (2.0 * DEPTH) ** 0.25
LN_EPS = 1e-5
RMS_EPS = 1e-6
SUBLN_EPS = 1e-5
LAMBDA_INIT0 = 0.8 - 0.6 * math.exp(-0.3 * 0)
TWO_PI = 2.0 * math.pi
C1 = 6.28125
C2 = float(np.float32(TWO_PI - 6.28125))
C3 = TWO_PI - 6.28125 - C2


def build_program(taps=()):
    nc = bass.Bass("TRN2", target_bir_lowering=False)
    S = Sched(nc)

    def din(name, shape, dt=F32):
        return nc.dram_tensor(name, list(shape), dt, kind="ExternalInput").ap()

    x_d = din("x", [S_LEN, D])
    cT_d = din("cT", [128, 8])
    pos_d = din("pos", [128, NTB], I32)
    adaw_d = din("ada_w", [2, D, 3 * D])
    adab_d = din("ada_b", [1, 2 * 3 * D])
    lng_d = din("ln_g", [2, D])
    lnb_d = din("ln_b", [2, D])
    w0h_d = din("w0h", [8, D, 512])
    lam_d = din("lamrow", [1, 256])
    subg_d = din("subg", [1, 128])
    w0o_d = din("w0o", [D, D])
    w1a_d = din("w1a", [D, 1824])
    gq_d = din("gq", [1, 512])
    gkv_d = din("gkv", [1, 256])
    wuqn_d = din("wuqn", [16, 512, 64])
    wuqr_d = din("wuqr", [512, 512])
    wukvn_d = din("wukvn", [16, 256, 64])
    wukvv_d = din("wukvv", [256, 1024])
    w1o_d = din("w1o", [D, D])
    idf_d = din("idf", [128, 128])
    invf0_d = din("invf0", [128, 8])
    invf1_d = din("invf1", [128, 16])
    maskb_d = din("maskb", [128, 1])
    y_d = nc.dram_tensor("y", [S_LEN, D], F32, kind="ExternalOutput").ap()
    x1_d = nc.dram_tensor("x1_scratch", [S_LEN, D], F32, kind=("ExternalOutput" if taps else "Internal")).ap()
    tap_d = {}
    for (nm, shp) in taps:
        tap_d[nm] = nc.dram_tensor(nm, list(shp), F32, kind="ExternalOutput").ap()

    sb = lambda name, shape, dt: nc.alloc_sbuf_tensor("s_" + name, shape, dt)
    RU = sb("RU", [128, 16640], BF16)
    XS = sb("XS", [128, 8, 1024], F32)
    RO = sb("RO", [128, NTB, 1024], BF16)
    RH = sb("RH", [128, 20512], BF16)
    WB = sb("WB", [128, 2, 8, 512], BF16)
    WS = sb("WS", [128, 2, 2, 512], F32)
    QRT = sb("QRT", [128, 4, 2048], BF16)
    PT = sb("PT", [128, 3, 512], BF16)
    QKS = sb("QKS", [128, 2, 512], BF16)
    TMP = sb("TMP", [128, 2, 512], F32)
    idf = sb("idf", [128, 128], F32)
    idb = sb("idb", [128, 128], BF16)
    ones_r = sb("ones_r", [1, 128], F32)
    mh = sb("mh", [128, 8], F32)
    maskb = sb("maskb", [128, 1], F32)
    cT = sb("cT", [128, 8], F32)
    cact = sb("cact", [128, 8], F32)
    arow = RH[0:1, 0:12288].bitcast(F32)
    grow = sb("grow", [1, 2 * D], F32)
    SCSH = sb("SCSH", [128, 2, 16], F32)
    posf = sb("posf", [128, NTB], F32)
    posi = sb("posi", [128, NTB], I32)
    invf0 = sb("invf0", [128, 8], F32)
    invf1 = sb("invf1", [128, 16], F32)
    CS0 = sb("CS0", [128, 2, NTB, 8], F32)
    CS1 = sb("CS1", [128, 2, NTB, 16], F32)
    ang = TMP[:].rearrange("p a b -> p (a b)").rearrange("p (a b) -> p a b", a=4)
    angi = sb("angi", [128, NTB * 16], I32)
    lamr = sb("lamr", [1, 256], F32)
    lamt = sb("lamt", [1, 8], F32)
    NL = sb("NL", [128, 1], F32)
    gsub = sb("gsub", [128, 128], F32)
    gqb = sb("gqb", [128, 512], F32)
    gkvb = sb("gkvb", [128, 256], F32)
    SM = sb("SM", [128, 192], F32)
    RT = sb("RT", [128, 2, 4, 32], F32)
    ps = nc.alloc_psum_tensor("ps", [128, 8, 512], F32)

    def psb(b):
        return ps[:, b, :].bitcast(BF16)

    P = lambda b: ("ps", b)
    A = S.op

    def dma(q, out, in_, r=(), w=(), key=None):
        return S.op(q, lambda e: e.dma_start(out=out, in_=in_), r, w, dma_key=key)

    dma("sp", idf[:], idf_d, w=["idf"])
    dma("sp", cT[:], cT_d, w=["cT"])
    dma("sp", posi[:], pos_d, w=["posi"])
    dma("sp", invf0[:], invf0_d, w=["invf0"])
    dma("sp", invf1[:], invf1_d, w=["invf1"])
    dma("sp", maskb[:], maskb_d, w=["maskb"])
    dma("sp", arow, adab_d, w=["arow"])
    dma("sp", lamr[:], lam_d, w=["lamr"])
    dma(Q2, gsub[:], subg_d.partition_broadcast(128), w=["gsub"])
    dma(Q2, gqb[:], gq_d.partition_broadcast(128), w=["gqb"])
    dma(Q2, gkvb[:], gkv_d.partition_broadcast(128), w=["gkvb"])
    A("dve", lambda e: e.tensor_copy(out=idb[:], in_=idf[:]), ["idf"], ["idb"])
    A("pool", lambda e: e.memset(ones_r[:], 1.0), [], ["ones_r"])
    A("pool", lambda e: e.memset(mh[:], -0.5), [], ["mh"])
    A("pool", lambda e: e.tensor_scalar(out=gsub[:], in0=gsub[:], scalar1=(1.0 - LAMBDA_INIT0) * math.sqrt(128.0), scalar2=None, op0=ALU.mult),
      ["gsub"], ["gsub"])
    A("pool", lambda e: e.tensor_scalar(out=gqb[:], in0=gqb[:], scalar1=math.sqrt(512.0), scalar2=None, op0=ALU.mult), ["gqb"], ["gqb"])
    A("pool", lambda e: e.tensor_scalar(out=gkvb[:], in0=gkvb[:], scalar1=math.sqrt(256.0), scalar2=None, op0=ALU.mult), ["gkvb"], ["gkvb"])

    A("dve", lambda e: e.tensor_copy(out=posf[:], in_=posi[:]), ["posi"], ["posf"])

    def rope_table(CS, invf, nf, ikey):
        n = NTB * nf
        a0 = ang[:, 0, 0:n]
        a1 = ang[:, 1, 0:n]
        a2 = ang[:, 2, 0:n]
        a3 = ang[:, 3, 0:n]
        v3 = lambda t: t.rearrange("p (a b) -> p a b", a=NTB)
        A("dve", lambda e: e.tensor_tensor(out=v3(a0), in0=posf[:].unsqueeze(2).to_broadcast([128, NTB, nf]),
                                           in1=invf[:].unsqueeze(1).to_broadcast([128, NTB, nf]), op=ALU.mult),
          ["posf", ikey], ["ang0"])
        A("dve", lambda e: e.tensor_scalar(out=a1, in0=a0, scalar1=1.0 / TWO_PI, scalar2=None, op0=ALU.mult), ["ang0"], ["ang1"])
        A("dve", lambda e: e.tensor_copy(out=angi[:, 0:n], in_=a1), ["ang1"], ["angi"])
        A("dve", lambda e: e.tensor_copy(out=a1, in_=angi[:, 0:n]), ["angi"], ["ang1"])
        A("dve", lambda e: e.scalar_tensor_tensor(out=a2, in0=a1, scalar=-C1, in1=a0, op0=ALU.mult, op1=ALU.add), ["ang1", "ang0"], ["ang2"])
        A("dve", lambda e: e.scalar_tensor_tensor(out=a3, in0=a1, scalar=-C2, in1=a2, op0=ALU.mult, op1=ALU.add), ["ang1", "ang2"], ["ang3"])
        A("dve", lambda e: e.scalar_tensor_tensor(out=a2, in0=a1, scalar=-C3, in1=a3, op0=ALU.mult, op1=ALU.add), ["ang1", "ang3"], ["ang2"])
        A("dve", lambda e: e.tensor_scalar(out=a3, in0=a2, scalar1=-math.pi, scalar2=math.pi, op0=ALU.max, op1=ALU.min), ["ang2"], ["ang3"])
        A("act", lambda e: e.activation(out=CS[:, 1], in_=v3(a3), func=AF.Sin), ["ang3"], ["CS"])
        A("dve", lambda e: e.tensor_scalar(out=a0, in0=a2, scalar1=math.pi / 2, scalar2=None, op0=ALU.add), ["ang2"], ["ang0"])
        A("dve", lambda e: e.tensor_scalar(out=a1, in0=a0, scalar1=math.pi, scalar2=-TWO_PI, op0=ALU.is_gt, op1=ALU.mult), ["ang0"], ["ang1"])
        A("dve", lambda e: e.tensor_tensor(out=a2, in0=a0, in1=a1, op=ALU.add), ["ang0", "ang1"], ["ang2"])
        A("dve", lambda e: e.tensor_scalar(out=a3, in0=a2, scalar1=-math.pi, scalar2=math.pi, op0=ALU.max, op1=ALU.min), ["ang2", "CS"], ["ang3"])
        A("act", lambda e: e.activation(out=CS[:, 0], in_=v3(a3), func=AF.Sin), ["ang3"], ["CS"])

    rope_table(CS0, invf0, 8, "invf0")
    rope_table(CS1, invf1, 16, "invf1")

    lv = lamr[:].rearrange("p (a b) -> p a b", a=4)
    A("dve", lambda e: e.tensor_tensor(out=lamr[0:1, 0:64], in0=lamr[0:1, 0:64], in1=lamr[0:1, 64:128], op=ALU.mult), ["lamr"], ["lamr"])
    A("dve", lambda e: e.tensor_tensor(out=lamr[0:1, 128:192], in0=lamr[0:1, 128:192], in1=lamr[0:1, 192:256], op=ALU.mult), ["lamr"], ["lamr"])
    A("dve", lambda e: e.tensor_reduce(out=lamt[0:1, 0:1], in_=lamr[0:1, 0:64], axis=mybir.AxisListType.X, op=ALU.add), ["lamr"], ["lamt"])
    A("dve", lambda e: e.tensor_reduce(out=lamt[0:1, 1:2], in_=lamr[0:1, 128:192], axis=mybir.AxisListType.X, op=ALU.add), ["lamr"], ["lamt"])
    A("act", lambda e: e.activation(out=lamt[0:1, 2:4], in_=lamt[0:1, 0:2], func=AF.Exp), ["lamt"], ["lamt"])
    A("dve", lambda e: e.tensor_tensor(out=lamt[0:1, 4:5], in0=lamt[0:1, 3:4], in1=lamt[0:1, 2:3], op=ALU.subtract), ["lamt"], ["lamt"])
    A("dve", lambda e: e.tensor_scalar(out=lamt[0:1, 5:6], in0=lamt[0:1, 4:5], scalar1=-LAMBDA_INIT0, scalar2=None, op0=ALU.add), ["lamt"], ["lamt"])
    A("pe", lambda e: e.matmul(ps[:, 7, 0:1], lhsT=ones_r[0:1, :], rhs=lamt[0:1, 5:6], start=True, stop=True), ["ones_r", "lamt"], [P(7)])
    A("act", lambda e: e.copy(out=NL[:], in_=ps[:, 7, 0:1]), [P(7)], ["NL"])

    A("act", lambda e: e.activation(out=cact[:], in_=cT[:], func=AF.Silu), ["cT"], ["cact"])
    ADS = RO[:].rearrange("p a b -> p (a b)").bitcast(F32)
    def ada(li):
        for k in range(8):
            sl = (li * 8 + k) % 2
            dma("sp" if k % 2 == 0 else Q2, ADS[:, sl * 3072:(sl + 1) * 3072], adaw_d[li, k * 128:(k + 1) * 128, :], w=[("ads", sl)])
            for n in range(6):
                A("pe", lambda e, sl=sl, n=n, k=k: e.matmul(ps[0:1, n, :], lhsT=cact[:, k:k + 1], rhs=ADS[:, sl * 3072 + n * 512: sl * 3072 + (n + 1) * 512],
                                                            start=(k == 0), stop=(k == 7)),
                  ["cact", ("ads", sl)], [P(n)])
        for n in range(6):
            o = li * 3072 + n * 512
            A("dve", lambda e, n=n, o=o: e.tensor_tensor(out=arow[0:1, o:o + 512], in0=ps[0:1, n, :], in1=arow[0:1, o:o + 512], op=ALU.add),
              [P(n), "arow"], ["arow"])
        o = li * 3072 + 1024
        A("dve", lambda e, o=o: e.tensor_scalar(out=arow[0:1, o:o + 1024], in0=arow[0:1, o:o + 1024], scalar1=1.0, scalar2=None, op0=ALU.add), ["arow"], ["arow"])
        for j in range(8):
            A("pe", lambda e, j=j, li=li: e.matmul(ps[:, 6, j:j + 1], lhsT=arow[0:1, li * 3072 + 1024 + j * 128: li * 3072 + 1024 + (j + 1) * 128],
                                                   rhs=ones_r[0:1, 0:1], start=True, stop=True), ["arow", "ones_r"], [P(6)])
            A("pe", lambda e, j=j, li=li: e.matmul(ps[:, 6, 8 + j:9 + j], lhsT=arow[0:1, li * 3072 + j * 128: li * 3072 + (j + 1) * 128],
                                                   rhs=ones_r[0:1, 0:1], start=True, stop=True), ["arow", "ones_r"], [P(6)])
        A("act", lambda e, li=li: e.copy(out=SCSH[:, li, :], in_=ps[:, 6, 0:16]), [P(6)], ["SCSH"])
        A("dve", lambda e, li=li: e.tensor_copy(out=grow[0:1, li * 1024:(li + 1) * 1024], in_=arow[0:1, li * 3072 + 2048: li * 3072 + 3072]), ["arow"], ["grow"])

    UT = RU[:, 0:16384].rearrange("p (c t) -> p c t", c=8)

    def phase_U(li, src_d, src_key):
        def u_load(g):
            for t in range(4):
                tb = g * 4 + t
                slot = (g % 2) * 4 + t
                dma("sp" if t % 2 == 0 else Q2, XS[:, slot, :], src_d[tb * 128:(tb + 1) * 128, :], r=[(src_key, tb)] if src_key else [], w=[("XS", slot)])
        u_load(0)
        for g in range(4):
            if g + 1 < 4:
                u_load(g + 1)
            for c in range(8):
                for t in range(4):
                    slot = (g % 2) * 4 + t
                    A("pe", lambda e, c=c, t=t, slot=slot: e.transpose(ps[:, c, t * 128:(t + 1) * 128], XS[:, slot, c * 128:(c + 1) * 128], idf[:]),
                      [("XS", slot), "idf"], [P(c)])
                A("act", lambda e, c=c, g=g: e.activation(out=UT[:, c, g * 512:(g + 1) * 512], in_=ps[:, c, :], func=AF.Identity,
                                                          bias=SCSH[:, li, 8 + c:9 + c], scale=SCSH[:, li, c:c + 1]),
                  [P(c), "SCSH"], [("UT", g)])

    def load_weights(src_fn, nk, slot, key, ncols=512):
        for kp in range(0, nk, 2):
            ws = load_weights.ctr % 2
            load_weights.ctr += 1
            n2 = min(2, nk - kp)
            dma("sp", WS[:, ws, 0:n2, 0:ncols], src_fn(kp, n2), w=[("WS", ws)])
            A("dve", lambda e, ws=ws, kp=kp, n2=n2: e.tensor_copy(out=WB[:, slot, kp:kp + n2, 0:ncols], in_=WS[:, ws, 0:n2, 0:ncols]),
              [("WS", ws)], [key])
    load_weights.ctr = 0

    def rope_ops(psrc, dst, cs_cos, cs_sin, ng, half, rbuf):
        x1 = psrc[:, :, 0:half]
        x2 = psrc[:, :, half:2 * half]
        cb = cs_cos.unsqueeze(1).to_broadcast([128, ng, half])
        sbk = cs_sin.unsqueeze(1).to_broadcast([128, ng, half])
        n = ng * half
        t = [RT[:, rbuf, i, 0:n].rearrange("p (a b) -> p a b", a=ng) for i in range(4)] if n <= 32 else None
        rk = ("RT", rbuf)
        return x1, x2, cb, sbk, t, rk

    S.alias([("TMP", 0), ("TMP", 1)], ["ang0", "ang1", "ang2", "ang3"])
    ada(0)
    phase_U(0, x_d, None)
    ada(1)
    def L0_bufs(b):
        base = b * 10256
        QK = RH[:, base:base + 4096].rearrange("p (a t) -> p a t", a=2)
        VA = RH[:, base + 4096:base + 4096 + 2064].rearrange("p (k e) -> p k e", k=16)
        G = RH[:, base + 6160:base + 10256].bitcast(F32).rearrange("p (k e) -> p k e", k=16)
        return QK, VA, G

    S.alias([("QK", 0), ("QK", 1), ("VA", 0), ("VA", 1), ("G", 0), ("G", 1), ("VA1c", 0), ("VA1c", 1)], ["arow"])
    for b in range(2):
        QK, VA, G = L0_bufs(b)
        A("pool", lambda e, VA=VA: e.memset(VA[:, :, 128:129], 1.0), [], [("VA1c", b)])

    S.alias([("RO", tb) for tb in range(NTB)], [("ads", 0), ("ads", 1)])
    QZ = [XS[:, 2 * b_:2 * b_ + 2, :].rearrange("p a b -> p (a b)").bitcast(BF16).rearrange("p (m t) -> p m t", m=2) for b_ in range(2)]
    S.alias([("QZ", 0), ("QZ", 1), ("QZz", 0), ("QZz", 1)], [("XS", i) for i in range(8)])
    for b_ in range(2):
        A("pool", lambda e, b_=b_: e.memset(QZ[b_][64:128, 0, :], 0.0), [], [("QZz", b_)])
        A("pool", lambda e, b_=b_: e.memset(QZ[b_][0:64, 1, :], 0.0), [], [("QZz", b_)])
    scale0 = 64 ** -0.5
    pvctr = [0]
    sctr = [0]
    CB = QRT[:].rearrange("p a t -> p (a t)")[:, 0:6144].bitcast(F32).rearrange("p (a n) -> p a n", a=3)
    WO = WB[:].rearrange("p s c n -> p (s c n)").rearrange("p (c n) -> p c n", c=8)

    def phase_C_pre(li, w_d):
        S.alias(["WO"], [("WB", 0), ("WB", 1), ("WH", 0), ("WH", 1)])
        S.alias(["CB", "CBg"], ["QRT"])
        for n in range(2):
            A("pe", lambda e, n=n: e.matmul(ps[:, 6 + n, :], lhsT=ones_r[0:1, :], rhs=grow[0:1, li * 1024 + n * 512: li * 1024 + (n + 1) * 512],
                                            start=True, stop=True), ["ones_r", "grow"], [P(6 + n)])
            A("act", lambda e, n=n: e.copy(out=CB[:, 0, n * 512:(n + 1) * 512], in_=ps[:, 6 + n, :]), [P(6 + n)], ["CB"])
        dma("sp", CB[:, 1, :], lng_d[li:li + 1, :].partition_broadcast(128), w=["CBg"], key="cbdma")
        dma("sp", CB[:, 2, :], lnb_d[li:li + 1, :].partition_broadcast(128), w=["CBg"], key="cbdma")
        for kp in range(0, 8, 2):
            for nh in range(2):
                ws = load_weights.ctr % 2
                load_weights.ctr += 1
                dma("sp", WS[:, ws, :, :], w_d[kp * 128:(kp + 2) * 128, nh * 512:(nh + 1) * 512].rearrange("(k p) n -> p k n", p=128), w=[("WS", ws)])
                A("dve", lambda e, ws=ws, kp=kp, nh=nh: e.tensor_tensor(out=WO[:, kp:kp + 2, nh * 512:(nh + 1) * 512], in0=WS[:, ws, :, :],
                                                                       in1=CB[:, 0, nh * 512:(nh + 1) * 512].unsqueeze(1).to_broadcast([128, 2, 512]), op=ALU.mult),
                  [("WS", ws), "CB"], ["WO"])

    def l0_load_w(h):
        load_weights(lambda kp, n2, h=h: w0h_d[h, kp * 128:(kp + n2) * 128, :].rearrange("(k p) n -> p k n", p=128), 8, h % 2, ("WB", h % 2))

    l0_load_w(0)
    for h in range(8):
        b = h % 2
        QK, VA, G = L0_bufs(b)
        if h + 1 < 8:
            l0_load_w(h + 1)

        def proj_mm(tb):
            pb = tb % 2
            for c in range(8):
                A("pe", lambda e, c=c, tb=tb, pb=pb, b=b: e.matmul(ps[:, pb, :], lhsT=UT[:, c, tb * 128:(tb + 1) * 128], rhs=WB[:, b, c, :],
                                                                  start=(c == 0), stop=(c == 7)),
                  [("UT", tb // 4), ("WB", b)], [P(pb)])

        def proj_evac(tb):
            pb = tb % 2
            src = ps[:, pb, 0:256].rearrange("p (g d) -> p g d", g=4)
            dst = QKS[:, pb, 0:256].rearrange("p (g d) -> p g d", g=4)
            x1, x2, cb, sbk, t, rk = rope_ops(src, dst, CS0[:, 0, tb, :], CS0[:, 1, tb, :], 4, 8, pb)
            A("dve", lambda e, x1=x1, cb=cb, t=t: e.tensor_tensor(out=t[0], in0=x1, in1=cb, op=ALU.mult), [P(pb), "CS"], [rk])
            A("dve", lambda e, x2=x2, sbk=sbk, t=t: e.tensor_tensor(out=t[1], in0=x2, in1=sbk, op=ALU.mult), [P(pb), "CS"], [rk])
            A("dve", lambda e, x2=x2, cb=cb, t=t: e.tensor_tensor(out=t[2], in0=x2, in1=cb, op=ALU.mult), [P(pb), "CS"], [rk])
            A("dve", lambda e, x1=x1, sbk=sbk, t=t: e.tensor_tensor(out=t[3], in0=x1, in1=sbk, op=ALU.mult), [P(pb), "CS"], [rk])
            A("dve", lambda e, dst=dst, t=t: e.tensor_tensor(out=dst[:, :, 0:8], in0=t[0], in1=t[1], op=ALU.subtract), [rk], [("QKSr", pb)])
            A("dve", lambda e, dst=dst, t=t: e.tensor_tensor(out=dst[:, :, 8:16], in0=t[2], in1=t[3], op=ALU.add), [rk], [("QKSr", pb)])
            A("act", lambda e, dst=dst, src=src: e.copy(out=dst[:, :, 16:64], in_=src[:, :, 16:64]), [P(pb)], [("QKSm", pb)])
            A("act", lambda e, VA=VA, tb=tb, pb=pb: e.copy(out=VA[:, tb, 0:128], in_=ps[:, pb, 256:384]), [P(pb)], [("VA", b)])
            A("act", lambda e, pb=pb: e.activation(out=TMP[:, pb, 0:128], in_=ps[:, pb, 384:512], func=AF.Silu), [P(pb)], [("TMP", pb)])
            A("pool", lambda e, G=G, tb=tb, pb=pb: e.tensor_tensor(out=G[:, tb, :], in0=TMP[:, pb, 0:128], in1=gsub[:], op=ALU.mult),
              [("TMP", pb), "gsub"], [("G", b)])

        def proj_tr(tb):
            pb = tb % 2
            tbk = 2 + pb
            for a in range(2):
                A("pe", lambda e, a=a, pb=pb, tbk=tbk: e.transpose(psb(tbk)[:, a * 128:(a + 1) * 128], QKS[:, pb, a * 128:(a + 1) * 128], idb[:]),
                  [("QKSr", pb), ("QKSm", pb), "idb"], [P(tbk)])
            A("dve", lambda e, QK=QK, tb=tb, tbk=tbk: e.tensor_copy(out=QK[:, 1, tb * 128:(tb + 1) * 128], in_=psb(tbk)[:, 128:256]),
              [P(tbk)], [("QK", b)])
            A("dve", lambda e, tb=tb, tbk=tbk, qz=QZ[b]: e.tensor_copy(out=qz[0:64, 0, tb * 128:(tb + 1) * 128], in_=psb(tbk)[0:64, 0:128]),
              [P(tbk)], [("QZ", b)])
            A("dve", lambda e, tb=tb, tbk=tbk, qz=QZ[b]: e.tensor_copy(out=qz[64:128, 1, tb * 128:(tb + 1) * 128], in_=psb(tbk)[64:128, 0:128]),
              [P(tbk)], [("QZ", b)])

        for tb in range(NTB):
            proj_mm(tb)
            proj_evac(tb)
            if tb >= 1:
                proj_tr(tb - 1)
        proj_tr(NTB - 1)
        if h == 7:
            phase_C_pre(0, w0o_d)

        its = [(j, kb) for j in range(8) for kb in range(2 * j + 2)]
        sbs = []
        pvbs = {}
        for (j, kb) in its:
            sbs.append(sctr[0] % 3)
            sctr[0] += 1
            if j not in pvbs:
                pvbs[j] = pvctr[0] % 2
                pvctr[0] += 1

        def att_S(i):
            j, kb = its[i]
            sb_ = sbs[i]
            off = 128 if kb == 2 * j + 1 else 0
            for m in range(2):
                A("pe", lambda e, m=m, kb=kb, j=j, off=off, sb_=sb_, QK=QK, qz=QZ[b]: e.matmul(
                    ps[:, sb_, m * 256 + off:(m + 1) * 256], lhsT=QK[:, 1, kb * 128:(kb + 1) * 128],
                    rhs=qz[:, m, j * 256 + off:(j + 1) * 256], start=True, stop=True),
                  [("QK", b), ("QZ", b), ("QZz", b)], [P(sb_)])

        def att_exp(i):
            j, kb = its[i]
            sb_ = sbs[i]
            off = 128 if kb == 2 * j + 1 else 0
            sin = ps[:, sb_, :].rearrange("p (m q) -> p m q", m=2)
            pk = [P(sb_)]
            ptv = PT[:, sb_, :].rearrange("p (m q) -> p m q", m=2)
            if kb >= 2 * j:
                A("act", lambda e, off=off, sin=sin, ptv=ptv: e.activation(out=ptv[:, :, off:256], in_=sin[:, :, off:256], func=AF.Exp, scale=scale0),
                  pk, [("PT", sb_)])
                A("dve", lambda e, off=off, sb_=sb_: e.memset(PT[64:128, sb_, :].rearrange("p (m q) -> p m q", m=2)[:, :, off:off + 64], 0.0),
                  [], [("PT", sb_)])
            else:
                A("act", lambda e, sb_=sb_: e.activation(out=PT[:, sb_, :], in_=ps[:, sb_, :], func=AF.Exp, scale=scale0), pk, [("PT", sb_)])

        def att_PV(i):
            j, kb = its[i]
            sb_ = sbs[i]
            off = 128 if kb == 2 * j + 1 else 0
            for qi in range(2):
                if qi == 0 and off:
                    continue
                bank = 4 + 2 * pvbs[j] + qi
                for m in range(2):
                    A("pe", lambda e, qi=qi, m=m, kb=kb, j=j, bank=bank, sb_=sb_, VA=VA: e.matmul(
                        ps[:, bank, m * 129:(m + 1) * 129], lhsT=PT[:, sb_, m * 256 + qi * 128: m * 256 + (qi + 1) * 128],
                        rhs=VA[:, kb, :], start=(kb == 0 and m == 0), stop=(kb == 2 * j + qi), skip_group_check=True),
                      [("PT", sb_), ("VA", b), ("VA1c", b)], [P(bank)])

        def o32_of(qi, j):
            c0 = 128 if j % 2 == 0 else 384
            return TMP[:, qi, c0:c0 + 128], ("O32", qi, j % 2)

        def att_evac1(j):
            for qi in range(2):
                bank = 4 + 2 * pvbs[j] + qi
                sm = SM[:, qi * 32 + (j % 2) * 16: qi * 32 + (j % 2) * 16 + 16]
                sk = ("SM", qi, j % 2)
                tk = ("TMP", qi)
                o32, ok = o32_of(qi, j)
                A("dve", lambda e, bank=bank, sm=sm: e.reciprocal(out=sm[:, 0:2], in_=ps[:, bank, 0:258].rearrange("p (m e) -> p m e", m=2)[:, :, 128:129].rearrange("p m e -> p (m e)")),
                  [P(bank)], [sk])
                A("dve", lambda e, sm=sm: e.tensor_tensor(out=sm[:, 2:3], in0=sm[:, 1:2], in1=NL[:], op=ALU.mult), [sk, "NL"], [sk])
                A("dve", lambda e, bank=bank, sm=sm, qi=qi: e.tensor_scalar(out=TMP[:, qi, 0:128], in0=ps[:, bank, 129:257], scalar1=sm[:, 2:3], scalar2=None, op0=ALU.mult),
                  [P(bank), sk], [tk])
                A("dve", lambda e, bank=bank, sm=sm, qi=qi, o32=o32: e.scalar_tensor_tensor(out=o32, in0=ps[:, bank, 0:128], scalar=sm[:, 0:1],
                                                                                             in1=TMP[:, qi, 0:128], op0=ALU.mult, op1=ALU.add),
                  [P(bank), sk, tk], [ok])
                A("dve", lambda e, sm=sm, qi=qi, o32=o32: e.scalar_tensor_tensor(out=TMP[:, qi, 256:384], in0=o32, scalar=1.0, in1=o32,
                                                                                  op0=ALU.mult, op1=ALU.mult, accum_out=sm[:, 3:4]),
                  [ok], [tk, sk])
                A("dve", lambda e, sm=sm: e.tensor_scalar(out=sm[:, 4:5], in0=sm[:, 3:4], scalar1=128.0 * SUBLN_EPS, scalar2=None, op0=ALU.add), [sk], [sk])
                A("pool", lambda e, sm=sm: e.tensor_tensor(out=sm[:, 5:6], in0=sm[:, 4:5], in1=mh[:, 0:1], op=ALU.pow), [sk, "mh"], [sk])

        def att_evac2(j):
            for qi in range(2):
                tb = 2 * j + qi
                sm = SM[:, qi * 32 + (j % 2) * 16: qi * 32 + (j % 2) * 16 + 16]
                sk = ("SM", qi, j % 2)
                o32, ok = o32_of(qi, j)
                A("dve", lambda e, sm=sm, o32=o32, tb=tb, h=h, G=G: e.scalar_tensor_tensor(out=RO[:, tb, h * 128:(h + 1) * 128], in0=o32, scalar=sm[:, 5:6],
                                                                                           in1=G[:, tb, :], op0=ALU.mult, op1=ALU.mult),
                  [ok, sk, ("G", b)], [("RO", tb)])

        att_S(0)
        att_S(1)
        for i in range(len(its)):
            j, kb = its[i]
            if kb == 0 and j >= 1:
                att_evac2(j - 1)
            att_exp(i)
            if i + 2 < len(its):
                att_S(i + 2)
            att_PV(i)
            if kb == 2 * j + 1:
                att_evac1(j)
        att_evac2(7)

    def phase_C(li, src_d, src_key, w_d, dst_d, dst_key):
        OT = RH[:, 6144:6144 + 2048].rearrange("p (s n) -> p s n", s=2)
        XN = RH[:, 8192:8192 + 8192].bitcast(F32)
        S.alias([("OT", 0), ("OT", 1), ("XN", 0), ("XN", 1), ("XN", 2), ("XN", 3)],
                [("QK", 0), ("QK", 1), ("VA", 0), ("VA", 1), ("G", 0), ("G", 1), ("VA1c", 0), ("VA1c", 1), "L1H",
                 ("KT", 0), ("KT", 1), ("QT", 0), ("QT", 1), ("KTr", 0), ("KTr", 1), ("QTr", 0), ("QTr", 1), "QNT", "KVNT"])
        def c_xload(tb):
            dma(Q2, XS[:, tb % 4, :], src_d[tb * 128:(tb + 1) * 128, :], r=[(src_key, tb)] if src_key else [], w=[("XS", tb % 4)])

        def c_tr(tb):
            s2 = tb % 2
            for f in range(8):
                A("pe", lambda e, f=f, tb=tb, s2=s2: e.transpose(psb(s2)[:, f * 128:(f + 1) * 128], RO[:, tb, f * 128:(f + 1) * 128], idb[:]),
                  [("RO", tb), "idb"], [P(s2)])
            A("act", lambda e, s2=s2: e.copy(out=OT[:, s2, :], in_=psb(s2)[:, :]), [P(s2)], [("OT", s2)])

        def c_mm(tb):
            s2 = tb % 2
            s4 = tb % 4
            for n in range(2):
                bank = 2 + 2 * s2 + n
                for f in range(8):
                    A("pe", lambda e, f=f, n=n, bank=bank, s2=s2: e.matmul(ps[:, bank, :], lhsT=OT[:, s2, f * 128:(f + 1) * 128], rhs=WO[:, f, n * 512:(n + 1) * 512],
                                                                          start=(f == 0), stop=(f == 7)),
                      [("OT", s2), "WO"], [P(bank)])
                A("dve", lambda e, n=n, bank=bank, s4=s4, tb=tb: e.scalar_tensor_tensor(out=XN[:, s4 * 1024 + n * 512: s4 * 1024 + (n + 1) * 512],
                                                                                       in0=XS[:, tb % 4, n * 512:(n + 1) * 512], scalar=ALPHA, in1=ps[:, bank, :],
                                                                                       op0=ALU.mult, op1=ALU.add),
                  [P(bank), ("XS", tb % 4)], [("XN", s4)])

        def c_vars(tb):
            s4 = tb % 4
            return s4, XN[:, s4 * 1024:(s4 + 1) * 1024], SM[:, 64 + s4 * 32: 64 + (s4 + 1) * 32], ("SM", 2 + s4)

        def c_A(tb):
            s4, xn, sm, sk = c_vars(tb)
            for n in range(2):
                A("dve", lambda e, xn=xn, n=n, sm=sm: e.bn_stats(out=sm[:, 8 + 6 * n: 14 + 6 * n], in_=xn[:, n * 512:(n + 1) * 512]), [("XN", s4)], [sk])
            A("dve", lambda e, sm=sm: e.bn_aggr(out=sm[:, 0:2], in_=sm[:, 8:20]), [sk], [sk])
            A("dve", lambda e, sm=sm: e.tensor_scalar(out=sm[:, 2:3], in0=sm[:, 1:2], scalar1=LN_EPS, scalar2=None, op0=ALU.add), [sk], [sk])
            A("pool", lambda e, sm=sm: e.tensor_tensor(out=sm[:, 3:4], in0=sm[:, 2:3], in1=mh[:, 0:1], op=ALU.pow), [sk, "mh"], [sk])

        def c_B(tb):
            s4, xn, sm, sk = c_vars(tb)
            A("dve", lambda e, sm=sm: e.scalar_tensor_tensor(out=sm[:, 4:5], in0=sm[:, 0:1], scalar=-1.0, in1=sm[:, 3:4], op0=ALU.mult, op1=ALU.mult),
              [sk], [sk])
            A("act", lambda e, xn=xn, sm=sm: e.activation(out=xn, in_=xn, func=AF.Identity, bias=sm[:, 4:5], scale=sm[:, 3:4]),
              [("XN", s4), sk], [("XN", s4)])

        def c_C(tb):
            s4, xn, sm, sk = c_vars(tb)
            oslot = 4 + tb % 4
            A("dve", lambda e, xn=xn: e.tensor_tensor(out=xn, in0=xn, in1=CB[:, 1, :], op=ALU.mult), [("XN", s4), "CBg"], [("XN", s4)])
            A("pool", lambda e, xn=xn, oslot=oslot: e.tensor_tensor(out=XS[:, oslot, :], in0=xn, in1=CB[:, 2, :], op=ALU.add), [("XN", s4), "CBg"], [("XS", oslot)])
            dma("sp", dst_d[tb * 128:(tb + 1) * 128, :], XS[:, oslot, :], r=[("XS", oslot)], w=[(dst_key, tb)] if dst_key else [], key=("out", oslot))

        c_xload(0)
        c_xload(1)
        c_tr(0)
        for t in range(NTB + 2):
            if t + 2 < NTB:
                c_xload(t + 2)
            if t + 1 < NTB:
                c_tr(t + 1)
            if t < NTB:
                c_mm(t)
                c_A(t)
            if 0 <= t - 1 < NTB:
                c_B(t - 1)
            if 0 <= t - 2 < NTB:
                c_C(t - 2)

    S.alias([("XS", i) for i in range(8)], [("QZ", 0), ("QZ", 1), ("QZz", 0), ("QZz", 1)])
    phase_C(0, x_d, None, w0o_d, x1_d, "x1d")

    phase_U(1, x1_d, "x1d")
    KT = [RH[:, i * 2048:(i + 1) * 2048] for i in range(2)]
    QT = [RH[:, 4096 + i * 2048: 4096 + (i + 1) * 2048] for i in range(2)]
    QNT = RH[:, 8192:16384].rearrange("p (c t) -> p c t", c=4)
    KVNT = RH[:, 16384:20480].rearrange("p (c t) -> p c t", c=2)
    S.alias(["L1H", ("KTr", 0), ("KTr", 1), ("KT", 0), ("KT", 1), ("QT", 0), ("QT", 1), ("QTr", 0), ("QTr", 1), "QNT", "KVNT"], [("OT", 0), ("OT", 1), ("XN", 0), ("XN", 1), ("XN", 2), ("XN", 3)])
    S.alias([("RO", tb) for tb in range(NTB)], [("RO", tb) for tb in range(NTB)])

    S.alias([("TMP", 0), ("TMP", 1)], [("O32", q_, p_) for q_ in range(2) for p_ in range(2)] + [("TMP", 0), ("TMP", 1)])
    S.alias([("SM", 0), ("SM", 1)], [("SM", q_, p_) for q_ in range(2) for p_ in range(2)])
    for i_ in range(2):
        kv_ = KT[i_][64:97, :].rearrange("p (k s) -> p k s", s=128)
        qv_ = QT[i_][64:97, :].rearrange("p (k s) -> p k s", s=128)
        A("pool", lambda e, kv_=kv_: e.memset(kv_[:, :, 0:64], 0.0), [], [("KTr", i_)])
        A("pool", lambda e, kv_=kv_: e.memset(kv_[:, :, 64:128], 1.0), [], [("KTr", i_)])
        A("pool", lambda e, qv_=qv_: e.memset(qv_[:, :, 0:64], -30000.0), [], [("QTr", i_)])
        A("pool", lambda e, qv_=qv_: e.memset(qv_[:, :, 64:128], 0.0), [], [("QTr", i_)])
    S.alias([("WB", 0), ("WB", 1)], ["WO"])
    ncols = [512, 288, 512, 512]
    coff = [0, 512, 800, 1312]
    def a_load(n):
        load_weights(lambda kp, n2, n=n: w1a_d[kp * 128:(kp + n2) * 128, coff[n]:coff[n] + ncols[n]].rearrange("(k p) n -> p k n", p=128),
                     8, n % 2, ("WB", n % 2), ncols=ncols[n])

    a_load(0)
    for n in range(4):
        wsl = n % 2
        if n + 1 < 4:
            a_load(n + 1)

        def a_mm(tb):
            pb = tb % 2
            for c in range(8):
                A("pe", lambda e, c=c, tb=tb, pb=pb, wsl=wsl, n=n: e.matmul(ps[:, pb, 0:ncols[n]], lhsT=UT[:, c, tb * 128:(tb + 1) * 128], rhs=WB[:, wsl, c, 0:ncols[n]],
                                                                           start=(c == 0), stop=(c == 7)),
                  [("UT", tb // 4), ("WB", wsl)], [P(pb)])

        def a_evac(tb):
            pb = tb % 2
            sm = SM[:, pb * 32:(pb + 1) * 32]
            sk = ("SM", pb)
            if n in (0, 1):
                W_ = 512 if n == 0 else 256
                gb = gqb if n == 0 else gkvb
                A("act", lambda e, pb=pb, W_=W_: e.copy(out=TMP[:, pb, 0:W_], in_=ps[:, pb, 0:W_]), [P(pb)], [("TMP", pb)])
                A("dve", lambda e, pb=pb, W_=W_, sm=sm: e.scalar_tensor_tensor(out=QKS[:, pb, 0:W_], in0=TMP[:, pb, 0:W_], scalar=1.0, in1=TMP[:, pb, 0:W_],
                                                                                op0=ALU.mult, op1=ALU.mult, accum_out=sm[:, 0:1]),
                  [("TMP", pb)], [("QKSm", pb), sk])
                A("dve", lambda e, sm=sm, W_=W_: e.tensor_scalar(out=sm[:, 1:2], in0=sm[:, 0:1], scalar1=W_ * RMS_EPS, scalar2=None, op0=ALU.add), [sk], [sk])
                A("pool", lambda e, sm=sm: e.tensor_tensor(out=sm[:, 2:3], in0=sm[:, 1:2], in1=mh[:, 0:1], op=ALU.pow), [sk, "mh"], [sk])
                if n == 1:
                    src = ps[:, pb, 256:288].rearrange("p (g d) -> p g d", g=1)
                    dst = QKS[:, pb, 256:288].rearrange("p (g d) -> p g d", g=1)
                    x1, x2, cb, sbk, t, rk = rope_ops(src, dst, CS1[:, 0, tb, :], CS1[:, 1, tb, :], 1, 16, pb)
                    A("dve", lambda e, x1=x1, cb=cb, t=t: e.tensor_tensor(out=t[0], in0=x1, in1=cb, op=ALU.mult), [P(pb), "CS"], [rk])
                    A("dve", lambda e, x2=x2, sbk=sbk, t=t: e.tensor_tensor(out=t[1], in0=x2, in1=sbk, op=ALU.mult), [P(pb), "CS"], [rk])
                    A("dve", lambda e, x2=x2, cb=cb, t=t: e.tensor_tensor(out=t[2], in0=x2, in1=cb, op=ALU.mult), [P(pb), "CS"], [rk])
                    A("dve", lambda e, x1=x1, sbk=sbk, t=t: e.tensor_tensor(out=t[3], in0=x1, in1=sbk, op=ALU.mult), [P(pb), "CS"], [rk])
                    A("dve", lambda e, dst=dst, t=t: e.tensor_tensor(out=dst[:, :, 0:16], in0=t[0], in1=t[1], op=ALU.subtract), [rk], [("QKSr", pb)])
                    A("dve", lambda e, dst=dst, t=t: e.tensor_tensor(out=dst[:, :, 16:32], in0=t[2], in1=t[3], op=ALU.add), [rk], [("QKSr", pb)])
            else:
                gc = (n - 2) * 512
                A("act", lambda e, pb=pb, tb=tb, gc=gc: e.activation(out=RO[:, tb, gc:gc + 512], in_=ps[:, pb, :], func=AF.Silu), [P(pb)], [("RO", tb)])

        def a_norm(tb):
            if n not in (0, 1):
                return
            pb = tb % 2
            sm = SM[:, pb * 32:(pb + 1) * 32]
            sk = ("SM", pb)
            W_ = 512 if n == 0 else 256
            gb = gqb if n == 0 else gkvb
            A("dve", lambda e, pb=pb, W_=W_, sm=sm, gb=gb: e.scalar_tensor_tensor(out=QKS[:, pb, 0:W_], in0=TMP[:, pb, 0:W_], scalar=sm[:, 2:3], in1=gb[:, 0:W_],
                                                                                   op0=ALU.mult, op1=ALU.mult),
              [("TMP", pb), sk, "gqb", "gkvb"], [("QKSm", pb)])

        def a_tr(tb):
            if n not in (0, 1):
                return
            pb = tb % 2
            tbk = 2 + pb
            W_ = 512 if n == 0 else 256
            nt = W_ // 128
            for a in range(nt):
                A("pe", lambda e, a=a, pb=pb, tbk=tbk: e.transpose(psb(tbk)[:, a * 128:(a + 1) * 128], QKS[:, pb, a * 128:(a + 1) * 128], idb[:]),
                  [("QKSm", pb), "idb"], [P(tbk)])
            if n == 1:
                A("pe", lambda e, pb=pb, tbk=tbk: e.transpose(psb(tbk)[0:32, 256:384], QKS[:, pb, 256:288], idb[:]), [("QKSr", pb), "idb"], [P(tbk)])
            dstT = QNT if n == 0 else KVNT
            A("dve", lambda e, dstT=dstT, tb=tb, tbk=tbk, nt=nt: e.tensor_copy(out=dstT[:, :, tb * 128:(tb + 1) * 128],
                                                                               in_=psb(tbk)[:, 0:nt * 128].rearrange("p (a t) -> p a t", a=nt)),
              [P(tbk)], ["QNT" if n == 0 else "KVNT"])
            if n == 1:
                for i in range(2):
                    A("dve", lambda e, i=i, tb=tb, tbk=tbk: e.tensor_copy(out=KT[i][64:96, tb * 128:(tb + 1) * 128], in_=psb(tbk)[0:32, 256:384]),
                      [P(tbk)], [("KTr", i)])

        for t in range(NTB + 2):
            if t < NTB:
                a_mm(t)
            if 0 <= t - 2 < NTB:
                a_tr(t - 2)
            if t < NTB:
                a_evac(t)
            if 0 <= t - 1 < NTB:
                a_norm(t - 1)

    VA1 = RU[:, 0:16640].rearrange("p (k h e) -> p k h e", k=16, h=16)
    S.alias(["VA1", "VA1c"], [("UT", g) for g in range(4)])
    S.alias(["QRT"], ["CB", "CBg"])
    A("pool", lambda e: e.memset(VA1[:, :, :, 64:65], 1.0), [], ["VA1c"])
    for nh in range(2):
        wsl = nh % 2
        load_weights(lambda kp, n2, nh=nh: wukvv_d[kp * 128:(kp + n2) * 128, nh * 512:(nh + 1) * 512].rearrange("(k p) n -> p k n", p=128), 2, wsl, ("WB", wsl))
        for tb in range(NTB):
            pb = tb % 2
            for c in range(2):
                A("pe", lambda e, c=c, tb=tb, pb=pb, wsl=wsl: e.matmul(ps[:, pb, :], lhsT=KVNT[:, c, tb * 128:(tb + 1) * 128], rhs=WB[:, wsl, c, :],
                                                                      start=(c == 0), stop=(c == 1)),
                  ["KVNT", ("WB", wsl)], [P(pb)])
            A("act", lambda e, tb=tb, pb=pb, nh=nh: e.copy(out=VA1[:, tb, nh * 8:(nh + 1) * 8, 0:64], in_=ps[:, pb, :].rearrange("p (h e) -> p h e", h=8)),
              [P(pb)], ["VA1"])
    load_weights(lambda kp, n2: wuqr_d[kp * 128:(kp + n2) * 128, :].rearrange("(k p) n -> p k n", p=128), 4, 0, ("WB", 0))
    def r_mm(tb):
        pb = tb % 2
        for c in range(4):
            A("pe", lambda e, c=c, tb=tb, pb=pb: e.matmul(ps[:, pb, :], lhsT=QNT[:, c, tb * 128:(tb + 1) * 128], rhs=WB[:, 0, c, :], start=(c == 0), stop=(c == 3)),
              ["QNT", ("WB", 0)], [P(pb)])
        src = ps[:, pb, :].rearrange("p (g d) -> p g d", g=16)
        dst = QKS[:, pb, :].rearrange("p (g d) -> p g d", g=16)
        x1, x2, cb, sbk, t, rk = rope_ops(src, dst, CS1[:, 0, tb, :], CS1[:, 1, tb, :], 16, 16, pb)
        if tb == 0:
            S.alias(["angT"], ["ang0", "ang1", "ang2", "ang3", ("TMP", 0), ("TMP", 1)])
        tt = [ang[:, i, 0:256].rearrange("p (a b) -> p a b", a=16) for i in range(4)]
        rk = "angT"
        A("dve", lambda e, x1=x1, cb=cb, tt=tt: e.tensor_tensor(out=tt[0], in0=x1, in1=cb, op=ALU.mult), [P(pb), "CS"], [rk])
        A("dve", lambda e, x2=x2, sbk=sbk, tt=tt: e.tensor_tensor(out=tt[1], in0=x2, in1=sbk, op=ALU.mult), [P(pb), "CS"], [rk])
        A("dve", lambda e, x2=x2, cb=cb, tt=tt: e.tensor_tensor(out=tt[2], in0=x2, in1=cb, op=ALU.mult), [P(pb), "CS"], [rk])
        A("dve", lambda e, x1=x1, sbk=sbk, tt=tt: e.tensor_tensor(out=tt[3], in0=x1, in1=sbk, op=ALU.mult), [P(pb), "CS"], [rk])
        A("dve", lambda e, dst=dst, tt=tt: e.tensor_tensor(out=dst[:, :, 0:16], in0=tt[0], in1=tt[1], op=ALU.subtract), [rk], [("QKSm", pb)])
        A("dve", lambda e, dst=dst, tt=tt: e.tensor_tensor(out=dst[:, :, 16:32], in0=tt[2], in1=tt[3], op=ALU.add), [rk], [("QKSm", pb)])

    def r_tr(tb):
        pb = tb % 2
        tbk = 2 + pb
        for a in range(4):
            A("pe", lambda e, a=a, pb=pb, tbk=tbk: e.transpose(psb(tbk)[:, a * 128:(a + 1) * 128], QKS[:, pb, a * 128:(a + 1) * 128], idb[:]),
              [("QKSm", pb), "idb"], [P(tbk)])
        A("dve", lambda e, tb=tb, tbk=tbk: e.tensor_copy(out=QRT[:, :, tb * 128:(tb + 1) * 128], in_=psb(tbk)[:, 0:512].rearrange("p (a t) -> p a t", a=4)),
          [P(tbk)], ["QRT"])

    for tb in range(NTB):
        r_mm(tb)
        if tb >= 1:
            r_tr(tb - 1)
    r_tr(NTB - 1)

    scale1 = 96 ** -0.5
    WH = WB[:, 1, :, :].rearrange("p c n -> p (c n)")
    S.alias([("WH", 0), ("WH", 1)], [("WB", 1)])
    def l1_proj(h):
        b = h % 2
        wq = WH[:, b * 512: b * 512 + 256].rearrange("p (c n) -> p c n", c=4)
        wk = WH[:, b * 512 + 256: b * 512 + 384].rearrange("p (c n) -> p c n", c=2)
        ws = load_weights.ctr % 2
        load_weights.ctr += 1
        dma("sp", WS[:, ws, 0, 0:256].rearrange("p (c n) -> p c n", c=4), wuqn_d[h].rearrange("(c p) n -> p c n", p=128), w=[("WS", ws)])
        dma("sp", WS[:, ws, 1, 0:128].rearrange("p (c n) -> p c n", c=2), wukvn_d[h].rearrange("(c p) n -> p c n", p=128), w=[("WS", ws)])
        A("pool", lambda e, ws=ws, wq=wq: e.tensor_copy(out=wq, in_=WS[:, ws, 0, 0:256].rearrange("p (c n) -> p c n", c=4)), [("WS", ws)], [("WH", b)])
        A("pool", lambda e, ws=ws, wk=wk: e.tensor_copy(out=wk, in_=WS[:, ws, 1, 0:128].rearrange("p (c n) -> p c n", c=2)), [("WS", ws)], [("WH", b)])
        for tt_ in range(4):
            pk_ = 4 + tt_ % 2
            for c in range(2):
                A("pe", lambda e, c=c, tt_=tt_, pk_=pk_, wk=wk: e.matmul(ps[0:64, pk_, :], lhsT=wk[:, c, :], rhs=KVNT[:, c, tt_ * 512:(tt_ + 1) * 512], start=(c == 0), stop=(c == 1)),
                  [("WH", b), "KVNT"], [P(pk_)])
            A("dve", lambda e, tt_=tt_, pk_=pk_, b=b: e.tensor_copy(out=KT[b][0:64, tt_ * 512:(tt_ + 1) * 512], in_=ps[0:64, pk_, :]), [P(pk_)], [("KT", b)])
            pq_ = 7
            for c in range(4):
                A("pe", lambda e, c=c, tt_=tt_, pq_=pq_, wq=wq: e.matmul(ps[0:64, pq_, :], lhsT=wq[:, c, :], rhs=QNT[:, c, tt_ * 512:(tt_ + 1) * 512], start=(c == 0), stop=(c == 3)),
                  [("WH", b), "QNT"], [P(pq_)])
            A("dve", lambda e, tt_=tt_, pq_=pq_, b=b: e.tensor_copy(out=QT[b][0:64, tt_ * 512:(tt_ + 1) * 512], in_=ps[0:64, pq_, :]), [P(pq_)], [("QT", b)])
        A("pool", lambda e, b=b, h=h: e.tensor_copy(out=QT[b][64:96, :], in_=QRT[(h % 4) * 32:(h % 4) * 32 + 32, h // 4, :]), ["QRT"], [("QTr", b)])

    l1_proj(0)
    for h in range(16):
        b = h % 2
        if h + 1 < 16:
            l1_proj(h + 1)
        if h + 1 == 15:
            phase_C_pre(1, w1o_d)
        its = [(j, kb) for j in range(4) for kb in range(4 * j + 4)]
        sbs = []
        pvbs = {}
        for (j, kb) in its:
            sbs.append((0, 1, 6)[sctr[0] % 3])
            sctr[0] += 1
            if j not in pvbs:
                pvbs[j] = 2 + pvctr[0] % 2
                pvctr[0] += 1

        def a1_S(i):
            j, kb = its[i]
            sb_ = sbs[i]
            d = kb - 4 * j
            off = 128 * d if d > 0 else 0
            keys_ = [("KT", b), ("KTr", b), ("QT", b), ("QTr", b)]
            if d >= 0:
                A("pe", lambda e, kb=kb, j=j, off=off, sb_=sb_, b=b: e.matmul(ps[:, sb_, off:off + 128], lhsT=KT[b][0:97, kb * 128:(kb + 1) * 128],
                                                                             rhs=QT[b][0:97, j * 512 + off: j * 512 + off + 128], start=True, stop=True), keys_, [P(sb_)])
                if off + 128 < 512:
                    A("pe", lambda e, kb=kb, j=j, off=off, sb_=sb_, b=b: e.matmul(ps[:, sb_, off + 128:512], lhsT=KT[b][0:96, kb * 128:(kb + 1) * 128],
                                                                                 rhs=QT[b][0:96, j * 512 + off + 128:(j + 1) * 512], start=True, stop=True), keys_, [P(sb_)])
            else:
                A("pe", lambda e, kb=kb, j=j, sb_=sb_, b=b: e.matmul(ps[:, sb_, :], lhsT=KT[b][0:96, kb * 128:(kb + 1) * 128],
                                                                    rhs=QT[b][0:96, j * 512:(j + 1) * 512], start=True, stop=True), keys_, [P(sb_)])

        def a1_exp(i):
            j, kb = its[i]
            sb_ = sbs[i]
            pt_ = (0, 1, 6).index(sb_)
            d = kb - 4 * j
            off = 128 * d if d > 0 else 0
            A("act", lambda e, sb_=sb_, pt_=pt_, off=off: e.activation(out=PT[:, pt_, off:512], in_=ps[:, sb_, off:512], func=AF.Exp, scale=scale1),
              [P(sb_)], [("PT", pt_)])

        def a1_PV(i):
            j, kb = its[i]
            sb_ = sbs[i]
            d = kb - 4 * j
            pvb = pvbs[j]
            pt_ = (0, 1, 6).index(sb_)
            for qi in range(max(d, 0), 4):
                A("pe", lambda e, qi=qi, kb=kb, pvb=pvb, pt_=pt_, h=h, j=j: e.matmul(
                    ps[:, pvb, qi * 65:(qi + 1) * 65], lhsT=PT[:, pt_, qi * 128:(qi + 1) * 128], rhs=VA1[:, kb, h, :],
                    start=(kb == 0 and qi == 0), stop=(kb == 4 * j + qi), skip_group_check=True),
                  [("PT", pt_), "VA1", "VA1c"], [P(pvb)])

        def a1_evac(j):
            pvb = pvbs[j]
            sm = SM[:, (j % 2) * 32:(j % 2 + 1) * 32]
            sk = ("SM", j % 2)
            A("dve", lambda e, pvb=pvb, sm=sm: e.reciprocal(out=sm[:, 0:4], in_=ps[:, pvb, 0:260].rearrange("p (q e) -> p q e", q=4)[:, :, 64:65].rearrange("p q e -> p (q e)")), [P(pvb)], [sk])
            for qi in range(4):
                tb = 4 * j + qi
                A("dve", lambda e, pvb=pvb, sm=sm, qi=qi, tb=tb, h=h: e.scalar_tensor_tensor(
                    out=RO[:, tb, h * 64:(h + 1) * 64], in0=ps[:, pvb, qi * 65: qi * 65 + 64], scalar=sm[:, qi:qi + 1],
                    in1=RO[:, tb, h * 64:(h + 1) * 64], op0=ALU.mult, op1=ALU.mult),
                  [P(pvb), sk, ("RO", tb)], [("RO", tb)])

        a1_S(0)
        a1_S(1)
        for i in range(len(its)):
            a1_exp(i)
            if i + 2 < len(its):
                a1_S(i + 2)
            a1_PV(i)
            j, kb = its[i]
            if kb == 4 * j + 3:
                a1_evac(j)

    phase_C(1, x1_d, "x1d", w1o_d, y_d, None)
    S.emit(final_wait_ops=[o for o in S.ops if o.is_dma])
    return nc, S


_CACHE = {}


def _host_inputs(x, c, positions, ada_w, ada_b, ln_g, ln_b, a_w_in, a_lambda_q1, a_lambda_k1, a_lambda_q2,
                 a_lambda_k2, a_subln_g, a_w_out, b_w_in, b_q_norm_g, b_w_uq, b_kv_norm_g, b_w_ukv, b_w_out):
    f = lambda a: np.ascontiguousarray(np.asarray(a, dtype=np.float32))
    a_w_in = f(a_w_in)[0]
    w0h = np.stack([np.concatenate([a_w_in[:, s * 1024 + h * 128: s * 1024 + (h + 1) * 128] for s in range(4)], axis=1)
                    for h in range(8)], axis=0)
    w_uq = f(b_w_uq)[0].reshape(512, 16, 96)
    w_ukv = f(b_w_ukv)[0].reshape(256, 16, 128)
    shared = {
        "ada_w": f(ada_w), "ada_b": f(ada_b).reshape(1, -1), "ln_g": f(ln_g), "ln_b": f(ln_b),
        "w0h": np.ascontiguousarray(w0h),
        "lamrow": np.concatenate([f(a_lambda_q1)[0], f(a_lambda_k1)[0], f(a_lambda_q2)[0], f(a_lambda_k2)[0]]).reshape(1, 256),
        "subg": f(a_subln_g).reshape(1, 128), "w0o": f(a_w_out)[0], "w1a": f(b_w_in)[0],
        "gq": f(b_q_norm_g).reshape(1, 512), "gkv": f(b_kv_norm_g).reshape(1, 256),
        "wuqn": np.ascontiguousarray(w_uq[:, :, 0:64].transpose(1, 0, 2)),
        "wuqr": np.ascontiguousarray(w_uq[:, :, 64:96].reshape(512, 512)),
        "wukvn": np.ascontiguousarray(w_ukv[:, :, 0:64].transpose(1, 0, 2)),
        "wukvv": np.ascontiguousarray(w_ukv[:, :, 64:128].reshape(256, 1024)),
        "w1o": f(b_w_out)[0],
        "idf": np.eye(128, dtype=np.float32),
        "invf0": np.tile((np.float32(500000.0) ** (-np.arange(0, 16, 2, dtype=np.float32) / np.float32(16))).astype(np.float32)[None, :], (128, 1)),
        "invf1": np.tile((np.float32(500000.0) ** (-np.arange(0, 32, 2, dtype=np.float32) / np.float32(32))).astype(np.float32)[None, :], (128, 1)),
        "maskb": np.concatenate([np.zeros(64, np.float32), np.full(64, -30000.0, np.float32)]).reshape(128, 1),
    }
    x = f(x)
    c = f(c)
    positions = np.asarray(positions, dtype=np.int32)
    maps = []
    for b in range(8):
        m = dict(shared)
        m["x"] = np.ascontiguousarray(x[b])
        m["cT"] = np.ascontiguousarray(c[b].reshape(8, 128).T)
        m["pos"] = np.ascontiguousarray(positions[b].reshape(16, 128).T)
        maps.append(m)
    return maps


def kernel(**inputs):
    if "nc" not in _CACHE:
        _CACHE["nc"] = build_program()[0]
    nc = _CACHE["nc"]
    maps = _host_inputs(**inputs)
    res = run_bass_kernel_spmd(nc, maps, core_ids=list(range(8)))
    return np.stack([np.asarray(r["y"], dtype=np.float32) for r in res.results], axis=0)
```

```python
import math
import numpy as np
import concourse.bass as bass
import concourse.mybir as mybir
from concourse.bass_utils import run_bass_kernel_spmd

F32 = mybir.dt.float32
BF16 = mybir.dt.bfloat16
I32 = mybir.dt.int32
AF = mybir.ActivationFunctionType
ALU = mybir.AluOpType


COMPUTE = ("pe", "act", "dve", "pool")
DMAQ = ("sp", "poolq")


class Op:
    __slots__ = ("eng", "fn", "deps", "raw_same", "signals", "sig", "sem", "waits",
                 "clock", "is_dma", "idx", "tag", "pos", "really")

    def __init__(self, eng, fn, tag=""):
        self.eng = eng
        self.fn = fn
        self.deps = []
        self.signals = False
        self.sig = 0
        self.sem = None
        self.waits = []
        self.clock = None
        self.is_dma = eng in DMAQ
        self.tag = tag


class Sched:
    def __init__(self, nc):
        self.nc = nc
        self.ops = []
        self.last_w = {}
        self.readers = {}
        self.dma_sem_of = {}
        self.dma_cnt = {}

    def op(self, eng, fn, reads=(), writes=(), dma_key=None, tag=""):
        o = Op(eng, fn, tag)
        o.idx = len(self.ops)
        reads = list(reads)
        writes = list(writes)
        deps = {}
        for k in reads:
            w = self.last_w.get(k)
            if w is not None:
                deps[w.idx] = (w, True)
            if isinstance(k, tuple) and k and k[0] == "ps":
                for r in self.readers.get(k, ()):
                    if r.eng != eng and r.idx not in deps:
                        deps[r.idx] = (r, False)
        for k in writes:
            w = self.last_w.get(k)
            if w is not None and w.idx not in deps:
                deps[w.idx] = (w, False)
            for r in self.readers.get(k, ()):
                if r.idx not in deps:
                    deps[r.idx] = (r, False)
        for k in writes:
            self.last_w[k] = o
            self.readers[k] = []
        for k in reads:
            self.readers.setdefault(k, []).append(o)
        for (x, raw) in deps.values():
            if x is o:
                continue
            if x.eng == o.eng and not x.is_dma:
                if o.eng == "pe" or (not raw and o.eng != "pool"):
                    continue
            o.deps.append(x)
            x.signals = True
        if o.is_dma:
            key = dma_key if dma_key is not None else (writes[0] if writes else reads[0])
            sname = self.dma_sem_of.setdefault(key, "dq%d" % len(self.dma_sem_of))
            self.dma_cnt[sname] = self.dma_cnt.get(sname, 0) + 16
            o.sem = sname
            o.sig = self.dma_cnt[sname]
        self.ops.append(o)
        return o

    def barrier_keys(self, keys):
        pass

    def emit(self, final_wait_ops=()):
        nc = self.nc
        cnt = {e: 0 for e in COMPUTE}
        for o in self.ops:
            o.really = o.is_dma
            if o.is_dma:
                o.pos = o.sig
                continue
            o.sem = "e_" + o.eng
            cnt[o.eng] += 1
            o.pos = cnt[o.eng]
        know = {e: {} for e in COMPUTE + DMAQ}
        for o in self.ops:
            K = know[o.eng]
            need = {}
            for x in o.deps:
                if x.pos > need.get(x.sem, (0, None))[0]:
                    need[x.sem] = (x.pos, x)
            waits = [(s_, x) for s_, (v, x) in need.items() if K.get(s_, 0) < v]
            for x in o.deps:
                if x.clock:
                    for s_, v in x.clock.items():
                        if K.get(s_, 0) < v:
                            K[s_] = v
            for s_, x in waits:
                x.really = True
                if K.get(s_, 0) < x.pos:
                    K[s_] = x.pos
            o.waits = waits
            if o.signals or o.is_dma:
                c = dict(K)
                c[o.sem] = max(c.get(o.sem, 0), o.pos)
                o.clock = c
        cnt = {e: 0 for e in COMPUTE}
        for o in self.ops:
            if o.is_dma:
                continue
            if o.really:
                cnt[o.eng] += 1
            o.sig = cnt[o.eng]
            o.signals = o.really
        for o in self.ops:
            o.waits = [(s_, x.sig) for s_, x in o.waits]
        self.n_incs = sum(1 for o in self.ops if o.signals and not o.is_dma)
        names = ["e_" + e for e in COMPUTE] + sorted(set(self.dma_cnt))
        self.n_waits = sum(len(o.waits) for o in self.ops)
        sems = {}
        import contextlib
        with contextlib.ExitStack() as st:
            for n in names:
                sems[n] = st.enter_context(nc.semaphore(n))
            block = st.enter_context(nc.Block())
            per = {e: [o for o in self.ops if o.eng == e] for e in COMPUTE + DMAQ}

            def run(engobj, lst, extra_final=()):
                for o in lst:
                    for s, v in o.waits:
                        engobj.wait_ge(sems[s], v)
                    ins = o.fn(engobj)
                    if o.is_dma:
                        ins.then_inc(sems[o.sem], 16)
                    elif o.signals:
                        ins.then_inc(sems[o.sem], 1)
                for (s, v) in extra_final:
                    engobj.wait_ge(sems[s], v)

            fmax = {}
            for o in final_wait_ops:
                fmax[o.sem] = max(fmax.get(o.sem, 0), o.sig)
            finals = sorted(fmax.items())

            @block.tensor
            def _(e):
                run(e, per["pe"])

            @block.scalar
            def _(e):
                run(e, per["act"])

            @block.vector
            def _(e):
                run(e, per["dve"])

            @block.gpsimd
            def _(e):
                merged = sorted(per["pool"] + per["poolq"], key=lambda o: o.idx)
                run(e, merged)

            @block.sync
            def _(e):
                run(e, per["sp"], finals)


def _alias(self, new, olds):
    ops = []
    for k in olds:
        w = self.last_w.get(k)
        if w is not None:
            ops.append(w)
        ops += self.readers.get(k, [])
    for k in new:
        self.last_w.pop(k, None)
        self.readers[k] = list(ops)


Sched.alias = _alias


import os
Q2 = os.environ.get("KQ2", "sp")
S_LEN = 2048
D = 1024
NTB = 16
DEPTH = 2
ALPHA = (2.0 * DEPTH) ** 0.25
LN_EPS = 1e-5
RMS_EPS = 1e-6
SUBLN_EPS = 1e-5
LAMBDA_INIT0 = 0.8 - 0.6 * math.exp(-0.3 * 0)
TWO_PI = 2.0 * math.pi
C1 = 6.28125
C2 = float(np.float32(TWO_PI - 6.28125))
C3 = TWO_PI - 6.28125 - C2


def build_program(taps=()):
    nc = bass.Bass("TRN2", target_bir_lowering=False)
    S = Sched(nc)

    def din(name, shape, dt=F32):
        return nc.dram_tensor(name, list(shape), dt, kind="ExternalInput").ap()

    x_d = din("x", [S_LEN, D])
    cT_d = din("cT", [128, 8])
    pos_d = din("pos", [128, NTB], I32)
    adaw_d = din("ada_w", [2, D, 3 * D])
    adab_d = din("ada_b", [1, 2 * 3 * D])
    lng_d = din("ln_g", [2, D])
    lnb_d = din("ln_b", [2, D])
    w0h_d = din("w0h", [8, D, 512])
    lam_d = din("lamrow", [1, 256])
    subg_d = din("subg", [1, 128])
    w0o_d = din("w0o", [D, D])
    w1a_d = din("w1a", [D, 1824])
    gq_d = din("gq", [1, 512])
    gkv_d = din("gkv", [1, 256])
    wuqn_d = din("wuqn", [16, 512, 64])
    wuqr_d = din("wuqr", [512, 512])
    wukvn_d = din("wukvn", [16, 256, 64])
    wukvv_d = din("wukvv", [256, 1024])
    w1o_d = din("w1o", [D, D])
    idf_d = din("idf", [128, 128])
    invf0_d = din("invf0", [128, 8])
    invf1_d = din("invf1", [128, 16])
    maskb_d = din("maskb", [128, 1])
    y_d = nc.dram_tensor("y", [S_LEN, D], F32, kind="ExternalOutput").ap()
    x1_d = nc.dram_tensor("x1_scratch", [S_LEN, D], F32, kind=("ExternalOutput" if taps else "Internal")).ap()
    tap_d = {}
    for (nm, shp) in taps:
        tap_d[nm] = nc.dram_tensor(nm, list(shp), F32, kind="ExternalOutput").ap()

    sb = lambda name, shape, dt: nc.alloc_sbuf_tensor("s_" + name, shape, dt)
    RU = sb("RU", [128, 16640], BF16)
    XS = sb("XS", [128, 8, 1024], F32)
    RO = sb("RO", [128, NTB, 1024], BF16)
    RH = sb("RH", [128, 20512], BF16)
    WB = sb("WB", [128, 2, 8, 512], BF16)
    WS = sb("WS", [128, 2, 2, 512], F32)
    QRT = sb("QRT", [128, 4, 2048], BF16)
    PT = sb("PT", [128, 3, 512], BF16)
    QKS = sb("QKS", [128, 2, 512], BF16)
    TMP = sb("TMP", [128, 2, 512], F32)
    idf = sb("idf", [128, 128], F32)
    idb = sb("idb", [128, 128], BF16)
    ones_r = sb("ones_r", [1, 128], F32)
    mh = sb("mh", [128, 8], F32)
    maskb = sb("maskb", [128, 1], F32)
    cT = sb("cT", [128, 8], F32)
    cact = sb("cact", [128, 8], F32)
    arow = RH[0:1, 0:12288].bitcast(F32)
    grow = sb("grow", [1, 2 * D], F32)
    SCSH = sb("SCSH", [128, 2, 16], F32)
    posf = sb("posf", [128, NTB], F32)
    posi = sb("posi", [128, NTB], I32)
    invf0 = sb("invf0", [128, 8], F32)
    invf1 = sb("invf1", [128, 16], F32)
    CS0 = sb("CS0", [128, 2, NTB, 8], F32)
    CS1 = sb("CS1", [128, 2, NTB, 16], F32)
    ang = TMP[:].rearrange("p a b -> p (a b)").rearrange("p (a b) -> p a b", a=4)
    angi = sb("angi", [128, NTB * 16], I32)
    lamr = sb("lamr", [1, 256], F32)
    lamt = sb("lamt", [1, 8], F32)
    NL = sb("NL", [128, 1], F32)
    gsub = sb("gsub", [128, 128], F32)
    gqb = sb("gqb", [128, 512], F32)
    gkvb = sb("gkvb", [128, 256], F32)
    SM = sb("SM", [128, 192], F32)
    RT = sb("RT", [128, 2, 4, 32], F32)
    ps = nc.alloc_psum_tensor("ps", [128, 8, 512], F32)

    def psb(b):
        return ps[:, b, :].bitcast(BF16)

    P = lambda b: ("ps", b)
    A = S.op

    def dma(q, out, in_, r=(), w=(), key=None):
        return S.op(q, lambda e: e.dma_start(out=out, in_=in_), r, w, dma_key=key)

    dma("sp", idf[:], idf_d, w=["idf"])
    dma("sp", cT[:], cT_d, w=["cT"])
    dma("sp", posi[:], pos_d, w=["posi"])
    dma("sp", invf0[:], invf0_d, w=["invf0"])
    dma("sp", invf1[:], invf1_d, w=["invf1"])
    dma("sp", maskb[:], maskb_d, w=["maskb"])
    dma("sp", arow, adab_d, w=["arow"])
    dma("sp", lamr[:], lam_d, w=["lamr"])
    dma(Q2, gsub[:], subg_d.partition_broadcast(128), w=["gsub"])
    dma(Q2, gqb[:], gq_d.partition_broadcast(128), w=["gqb"])
    dma(Q2, gkvb[:], gkv_d.partition_broadcast(128), w=["gkvb"])
    A("dve", lambda e: e.tensor_copy(out=idb[:], in_=idf[:]), ["idf"], ["idb"])
    A("pool", lambda e: e.memset(ones_r[:], 1.0), [], ["ones_r"])
    A("pool", lambda e: e.memset(mh[:], -0.5), [], ["mh"])
    A("pool", lambda e: e.tensor_scalar(out=gsub[:], in0=gsub[:], scalar1=(1.0 - LAMBDA_INIT0) * math.sqrt(128.0), scalar2=None, op0=ALU.mult),
      ["gsub"], ["gsub"])
    A("pool", lambda e: e.tensor_scalar(out=gqb[:], in0=gqb[:], scalar1=math.sqrt(512.0), scalar2=None, op0=ALU.mult), ["gqb"], ["gqb"])
    A("pool", lambda e: e.tensor_scalar(out=gkvb[:], in0=gkvb[:], scalar1=math.sqrt(256.0), scalar2=None, op0=ALU.mult), ["gkvb"], ["gkvb"])

    A("dve", lambda e: e.tensor_copy(out=posf[:], in_=posi[:]), ["posi"], ["posf"])

    def rope_table(CS, invf, nf, ikey):
        n = NTB * nf
        a0 = ang[:, 0, 0:n]
        a1 = ang[:, 1, 0:n]
        a2 = ang[:, 2, 0:n]
        a3 = ang[:, 3, 0:n]
        v3 = lambda t: t.rearrange("p (a b) -> p a b", a=NTB)
        A("dve", lambda e: e.tensor_tensor(out=v3(a0), in0=posf[:].unsqueeze(2).to_broadcast([128, NTB, nf]),
                                           in1=invf[:].unsqueeze(1).to_broadcast([128, NTB, nf]), op=ALU.mult),
          ["posf", ikey], ["ang0"])
        A("dve", lambda e: e.tensor_scalar(out=a1, in0=a0, scalar1=1.0 / TWO_PI, scalar2=None, op0=ALU.mult), ["ang0"], ["ang1"])
        A("dve", lambda e: e.tensor_copy(out=angi[:, 0:n], in_=a1), ["ang1"], ["angi"])
        A("dve", lambda e: e.tensor_copy(out=a1, in_=angi[:, 0:n]), ["angi"], ["ang1"])
        A("dve", lambda e: e.scalar_tensor_tensor(out=a2, in0=a1, scalar=-C1, in1=a0, op0=ALU.mult, op1=ALU.add), ["ang1", "ang0"], ["ang2"])
        A("dve", lambda e: e.scalar_tensor_tensor(out=a3, in0=a1, scalar=-C2, in1=a2, op0=ALU.mult, op1=ALU.add), ["ang1", "ang2"], ["ang3"])
        A("dve", lambda e: e.scalar_tensor_tensor(out=a2, in0=a1, scalar=-C3, in1=a3, op0=ALU.mult, op1=ALU.add), ["ang1", "ang3"], ["ang2"])
        A("dve", lambda e: e.tensor_scalar(out=a3, in0=a2, scalar1=-math.pi, scalar2=math.pi, op0=ALU.max, op1=ALU.min), ["ang2"], ["ang3"])
        A("act", lambda e: e.activation(out=CS[:, 1], in_=v3(a3), func=AF.Sin), ["ang3"], ["CS"])
        A("dve", lambda e: e.tensor_scalar(out=a0, in0=a2, scalar1=math.pi / 2, scalar2=None, op0=ALU.add), ["ang2"], ["ang0"])
        A("dve", lambda e: e.tensor_scalar(out=a1, in0=a0, scalar1=math.pi, scalar2=-TWO_PI, op0=ALU.is_gt, op1=ALU.mult), ["ang0"], ["ang1"])
        A("dve", lambda e: e.tensor_tensor(out=a2, in0=a0, in1=a1, op=ALU.add), ["ang0", "ang1"], ["ang2"])
        A("dve", lambda e: e.tensor_scalar(out=a3, in0=a2, scalar1=-math.pi, scalar2=math.pi, op0=ALU.max, op1=ALU.min), ["ang2", "CS"], ["ang3"])
        A("act", lambda e: e.activation(out=CS[:, 0], in_=v3(a3), func=AF.Sin), ["ang3"], ["CS"])

    rope_table(CS0, invf0, 8, "invf0")
    rope_table(CS1, invf1, 16, "invf1")

    lv = lamr[:].rearrange("p (a b) -> p a b", a=4)
    A("dve", lambda e: e.tensor_tensor(out=lamr[0:1, 0:64], in0=lamr[0:1, 0:64], in1=lamr[0:1, 64:128], op=ALU.mult), ["lamr"], ["lamr"])
    A("dve", lambda e: e.tensor_tensor(out=lamr[0:1, 128:192], in0=lamr[0:1, 128:192], in1=lamr[0:1, 192:256], op=ALU.mult), ["lamr"], ["lamr"])
    A("dve", lambda e: e.tensor_reduce(out=lamt[0:1, 0:1], in_=lamr[0:1, 0:64], axis=mybir.AxisListType.X, op=ALU.add), ["lamr"], ["lamt"])
    A("dve", lambda e: e.tensor_reduce(out=lamt[0:1, 1:2], in_=lamr[0:1, 128:192], axis=mybir.AxisListType.X, op=ALU.add), ["lamr"], ["lamt"])
    A("act", lambda e: e.activation(out=lamt[0:1, 2:4], in_=lamt[0:1, 0:2], func=AF.Exp), ["lamt"], ["lamt"])
    A("dve", lambda e: e.tensor_tensor(out=lamt[0:1, 4:5], in0=lamt[0:1, 3:4], in1=lamt[0:1, 2:3], op=ALU.subtract), ["lamt"], ["lamt"])
    A("dve", lambda e: e.tensor_scalar(out=lamt[0:1, 5:6], in0=lamt[0:1, 4:5], scalar1=-LAMBDA_INIT0, scalar2=None, op0=ALU.add), ["lamt"], ["lamt"])
    A("pe", lambda e: e.matmul(ps[:, 7, 0:1], lhsT=ones_r[0:1, :], rhs=lamt[0:1, 5:6], start=True, stop=True), ["ones_r", "lamt"], [P(7)])
    A("act", lambda e: e.copy(out=NL[:], in_=ps[:, 7, 0:1]), [P(7)], ["NL"])

    A("act", lambda e: e.activation(out=cact[:], in_=cT[:], func=AF.Silu), ["cT"], ["cact"])
    ADS = RO[:].rearrange("p a b -> p (a b)").bitcast(F32)
    def ada(li):
        for k in range(8):
            sl = (li * 8 + k) % 2
            dma("sp" if k % 2 == 0 else Q2, ADS[:, sl * 3072:(sl + 1) * 3072], adaw_d[li, k * 128:(k + 1) * 128, :], w=[("ads", sl)])
            for n in range(6):
                A("pe", lambda e, sl=sl, n=n, k=k: e.matmul(ps[0:1, n, :], lhsT=cact[:, k:k + 1], rhs=ADS[:, sl * 3072 + n * 512: sl * 3072 + (n + 1) * 512],
                                                            start=(k == 0), stop=(k == 7)),
                  ["cact", ("ads", sl)], [P(n)])
        for n in range(6):
            o = li * 3072 + n * 512
            A("dve", lambda e, n=n, o=o: e.tensor_tensor(out=arow[0:1, o:o + 512], in0=ps[0:1, n, :], in1=arow[0:1, o:o + 512], op=ALU.add),
              [P(n), "arow"], ["arow"])
        o = li * 3072 + 1024
        A("dve", lambda e, o=o: e.tensor_scalar(out=arow[0:1, o:o + 1024], in0=arow[0:1, o:o + 1024], scalar1=1.0, scalar2=None, op0=ALU.add), ["arow"], ["arow"])
        for j in range(8):
            A("pe", lambda e, j=j, li=li: e.matmul(ps[:, 6, j:j + 1], lhsT=arow[0:1, li * 3072 + 1024 + j * 128: li * 3072 + 1024 + (j + 1) * 128],
                                                   rhs=ones_r[0:1, 0:1], start=True, stop=True), ["arow", "ones_r"], [P(6)])
            A("pe", lambda e, j=j, li=li: e.matmul(ps[:, 6, 8 + j:9 + j], lhsT=arow[0:1, li * 3072 + j * 128: li * 3072 + (j + 1) * 128],
                                                   rhs=ones_r[0:1, 0:1], start=True, stop=True), ["arow", "ones_r"], [P(6)])
        A("act", lambda e, li=li: e.copy(out=SCSH[:, li, :], in_=ps[:, 6, 0:16]), [P(6)], ["SCSH"])
        A("dve", lambda e, li=li: e.tensor_copy(out=grow[0:1, li * 1024:(li + 1) * 1024], in_=arow[0:1, li * 3072 + 2048: li * 3072 + 3072]), ["arow"], ["grow"])

    UT = RU[:, 0:16384].rearrange("p (c t) -> p c t", c=8)

    def phase_U(li, src_d, src_key):
        def u_load(g):
            for t in range(4):
                tb = g * 4 + t
                slot = (g % 2) * 4 + t
                dma("sp" if t % 2 == 0 else Q2, XS[:, slot, :], src_d[tb * 128:(tb + 1) * 128, :], r=[(src_key, tb)] if src_key else [], w=[("XS", slot)])
        u_load(0)
        for g in range(4):
            if g + 1 < 4:
                u_load(g + 1)
            for c in range(8):
                for t in range(4):
                    slot = (g % 2) * 4 + t
                    A("pe", lambda e, c=c, t=t, slot=slot: e.transpose(ps[:, c, t * 128:(t + 1) * 128], XS[:, slot, c * 128:(c + 1) * 128], idf[:]),
                      [("XS", slot), "idf"], [P(c)])
                A("act", lambda e, c=c, g=g: e.activation(out=UT[:, c, g * 512:(g + 1) * 512], in_=ps[:, c, :], func=AF.Identity,
                                                          bias=SCSH[:, li, 8 + c:9 + c], scale=SCSH[:, li, c:c + 1]),
                  [P(c), "SCSH"], [("UT", g)])

    def load_weights(src_fn, nk, slot, key, ncols=512):
        for kp in range(0, nk, 2):
            ws = load_weights.ctr % 2
            load_weights.ctr += 1
            n2 = min(2, nk - kp)
            dma("sp", WS[:, ws, 0:n2, 0:ncols], src_fn(kp, n2), w=[("WS", ws)])
            A("dve", lambda e, ws=ws, kp=kp, n2=n2: e.tensor_copy(out=WB[:, slot, kp:kp + n2, 0:ncols], in_=WS[:, ws, 0:n2, 0:ncols]),
              [("WS", ws)], [key])
    load_weights.ctr = 0

    def rope_ops(psrc, dst, cs_cos, cs_sin, ng, half, rbuf):
        x1 = psrc[:, :, 0:half]
        x2 = psrc[:, :, half:2 * half]
        cb = cs_cos.unsqueeze(1).to_broadcast([128, ng, half])
        sbk = cs_sin.unsqueeze(1).to_broadcast([128, ng, half])
        n = ng * half
        t = [RT[:, rbuf, i, 0:n].rearrange("p (a b) -> p a b", a=ng) for i in range(4)] if n <= 32 else None
        rk = ("RT", rbuf)
        return x1, x2, cb, sbk, t, rk

    S.alias([("TMP", 0), ("TMP", 1)], ["ang0", "ang1", "ang2", "ang3"])
    ada(0)
    phase_U(0, x_d, None)
    ada(1)
    def L0_bufs(b):
        base = b * 10256
        QK = RH[:, base:base + 4096].rearrange("p (a t) -> p a t", a=2)
        VA = RH[:, base + 4096:base + 4096 + 2064].rearrange("p (k e) -> p k e", k=16)
        G = RH[:, base + 6160:base + 10256].bitcast(F32).rearrange("p (k e) -> p k e", k=16)
        return QK, VA, G

    S.alias([("QK", 0), ("QK", 1), ("VA", 0), ("VA", 1), ("G", 0), ("G", 1), ("VA1c", 0), ("VA1c", 1)], ["arow"])
    for b in range(2):
        QK, VA, G = L0_bufs(b)
        A("pool", lambda e, VA=VA: e.memset(VA[:, :, 128:129], 1.0), [], [("VA1c", b)])

    S.alias([("RO", tb) for tb in range(NTB)], [("ads", 0), ("ads", 1)])
    QZ = [XS[:, 2 * b_:2 * b_ + 2, :].rearrange("p a b -> p (a b)").bitcast(BF16).rearrange("p (m t) -> p m t", m=2) for b_ in range(2)]
    S.alias([("QZ", 0), ("QZ", 1), ("QZz", 0), ("QZz", 1)], [("XS", i) for i in range(8)])
    for b_ in range(2):
        A("pool", lambda e, b_=b_: e.memset(QZ[b_][64:128, 0, :], 0.0), [], [("QZz", b_)])
        A("pool", lambda e, b_=b_: e.memset(QZ[b_][0:64, 1, :], 0.0), [], [("QZz", b_)])
    scale0 = 64 ** -0.5
    pvctr = [0]
    sctr = [0]
    CB = QRT[:].rearrange("p a t -> p (a t)")[:, 0:6144].bitcast(F32).rearrange("p (a n) -> p a n", a=3)
    WO = WB[:].rearrange("p s c n -> p (s c n)").rearrange("p (c n) -> p c n", c=8)

    def phase_C_pre(li, w_d):
        S.alias(["WO"], [("WB", 0), ("WB", 1), ("WH", 0), ("WH", 1)])
        S.alias(["CB", "CBg"], ["QRT"])
        for n in range(2):
            A("pe", lambda e, n=n: e.matmul(ps[:, 6 + n, :], lhsT=ones_r[0:1, :], rhs=grow[0:1, li * 1024 + n * 512: li * 1024 + (n + 1) * 512],
                                            start=True, stop=True), ["ones_r", "grow"], [P(6 + n)])
            A("act", lambda e, n=n: e.copy(out=CB[:, 0, n * 512:(n + 1) * 512], in_=ps[:, 6 + n, :]), [P(6 + n)], ["CB"])
        dma("sp", CB[:, 1, :], lng_d[li:li + 1, :].partition_broadcast(128), w=["CBg"], key="cbdma")
        dma("sp", CB[:, 2, :], lnb_d[li:li + 1, :].partition_broadcast(128), w=["CBg"], key="cbdma")
        for kp in range(0, 8, 2):
            for nh in range(2):
                ws = load_weights.ctr % 2
                load_weights.ctr += 1
                dma("sp", WS[:, ws, :, :], w_d[kp * 128:(kp + 2) * 128, nh * 512:(nh + 1) * 512].rearrange("(k p) n -> p k n", p=128), w=[("WS", ws)])
                A("dve", lambda e, ws=ws, kp=kp, nh=nh: e.tensor_tensor(out=WO[:, kp:kp + 2, nh * 512:(nh + 1) * 512], in0=WS[:, ws, :, :],
                                                                       in1=CB[:, 0, nh * 512:(nh + 1) * 512].unsqueeze(1).to_broadcast([128, 2, 512]), op=ALU.mult),
                  [("WS", ws), "CB"], ["WO"])

    def l0_load_w(h):
        load_weights(lambda kp, n2, h=h: w0h_d[h, kp * 128:(kp + n2) * 128, :].rearrange("(k p) n -> p k n", p=128), 8, h % 2, ("WB", h % 2))

    l0_load_w(0)
    for h in range(8):
        b = h % 2
        QK, VA, G = L0_bufs(b)
        if h + 1 < 8:
            l0_load_w(h + 1)

        def proj_mm(tb):
            pb = tb % 2
            for c in range(8):
                A("pe", lambda e, c=c, tb=tb, pb=pb, b=b: e.matmul(ps[:, pb, :], lhsT=UT[:, c, tb * 128:(tb + 1) * 128], rhs=WB[:, b, c, :],
                                                                  start=(c == 0), stop=(c == 7)),
                  [("UT", tb // 4), ("WB", b)], [P(pb)])

        def proj_evac(tb):
            pb = tb % 2
            src = ps[:, pb, 0:256].rearrange("p (g d) -> p g d", g=4)
            dst = QKS[:, pb, 0:256].rearrange("p (g d) -> p g d", g=4)
            x1, x2, cb, sbk, t, rk = rope_ops(src, dst, CS0[:, 0, tb, :], CS0[:, 1, tb, :], 4, 8, pb)
            A("dve", lambda e, x1=x1, cb=cb, t=t: e.tensor_tensor(out=t[0], in0=x1, in1=cb, op=ALU.mult), [P(pb), "CS"], [rk])
            A("dve", lambda e, x2=x2, sbk=sbk, t=t: e.tensor_tensor(out=t[1], in0=x2, in1=sbk, op=ALU.mult), [P(pb), "CS"], [rk])
            A("dve", lambda e, x2=x2, cb=cb, t=t: e.tensor_tensor(out=t[2], in0=x2, in1=cb, op=ALU.mult), [P(pb), "CS"], [rk])
            A("dve", lambda e, x1=x1, sbk=sbk, t=t: e.tensor_tensor(out=t[3], in0=x1, in1=sbk, op=ALU.mult), [P(pb), "CS"], [rk])
            A("dve", lambda e, dst=dst, t=t: e.tensor_tensor(out=dst[:, :, 0:8], in0=t[0], in1=t[1], op=ALU.subtract), [rk], [("QKSr", pb)])
            A("dve", lambda e, dst=dst, t=t: e.tensor_tensor(out=dst[:, :, 8:16], in0=t[2], in1=t[3], op=ALU.add), [rk], [("QKSr", pb)])
            A("act", lambda e, dst=dst, src=src: e.copy(out=dst[:, :, 16:64], in_=src[:, :, 16:64]), [P(pb)], [("QKSm", pb)])
            A("act", lambda e, VA=VA, tb=tb, pb=pb: e.copy(out=VA[:, tb, 0:128], in_=ps[:, pb, 256:384]), [P(pb)], [("VA", b)])
            A("act", lambda e, pb=pb: e.activation(out=TMP[:, pb, 0:128], in_=ps[:, pb, 384:512], func=AF.Silu), [P(pb)], [("TMP", pb)])
            A("pool", lambda e, G=G, tb=tb, pb=pb: e.tensor_tensor(out=G[:, tb, :], in0=TMP[:, pb, 0:128], in1=gsub[:], op=ALU.mult),
              [("TMP", pb), "gsub"], [("G", b)])

        def proj_tr(tb):
            pb = tb % 2
            tbk = 2 + pb
            for a in range(2):
                A("pe", lambda e, a=a, pb=pb, tbk=tbk: e.transpose(psb(tbk)[:, a * 128:(a + 1) * 128], QKS[:, pb, a * 128:(a + 1) * 128], idb[:]),
                  [("QKSr", pb), ("QKSm", pb), "idb"], [P(tbk)])
            A("dve", lambda e, QK=QK, tb=tb, tbk=tbk: e.tensor_copy(out=QK[:, 1, tb * 128:(tb + 1) * 128], in_=psb(tbk)[:, 128:256]),
              [P(tbk)], [("QK", b)])
            A("dve", lambda e, tb=tb, tbk=tbk, qz=QZ[b]: e.tensor_copy(out=qz[0:64, 0, tb * 128:(tb + 1) * 128], in_=psb(tbk)[0:64, 0:128]),
              [P(tbk)], [("QZ", b)])
            A("dve", lambda e, tb=tb, tbk=tbk, qz=QZ[b]: e.tensor_copy(out=qz[64:128, 1, tb * 128:(tb + 1) * 128], in_=psb(tbk)[64:128, 0:128]),
              [P(tbk)], [("QZ", b)])

        for tb in range(NTB):
            proj_mm(tb)
            proj_evac(tb)
            if tb >= 1:
                proj_tr(tb - 1)
        proj_tr(NTB - 1)
        if h == 7:
            phase_C_pre(0, w0o_d)

        its = [(j, kb) for j in range(8) for kb in range(2 * j + 2)]
        sbs = []
        pvbs = {}
        for (j, kb) in its:
            sbs.append(sctr[0] % 3)
            sctr[0] += 1
            if j not in pvbs:
                pvbs[j] = pvctr[0] % 2
                pvctr[0] += 1

        def att_S(i):
            j, kb = its[i]
            sb_ = sbs[i]
            off = 128 if kb == 2 * j + 1 else 0
            for m in range(2):
                A("pe", lambda e, m=m, kb=kb, j=j, off=off, sb_=sb_, QK=QK, qz=QZ[b]: e.matmul(
                    ps[:, sb_, m * 256 + off:(m + 1) * 256], lhsT=QK[:, 1, kb * 128:(kb + 1) * 128],
                    rhs=qz[:, m, j * 256 + off:(j + 1) * 256], start=True, stop=True),
                  [("QK", b), ("QZ", b), ("QZz", b)], [P(sb_)])

        def att_exp(i):
            j, kb = its[i]
            sb_ = sbs[i]
            off = 128 if kb == 2 * j + 1 else 0
            sin = ps[:, sb_, :].rearrange("p (m q) -> p m q", m=2)
            pk = [P(sb_)]
            ptv = PT[:, sb_, :].rearrange("p (m q) -> p m q", m=2)
            if kb >= 2 * j:
                A("act", lambda e, off=off, sin=sin, ptv=ptv: e.activation(out=ptv[:, :, off:256], in_=sin[:, :, off:256], func=AF.Exp, scale=scale0),
                  pk, [("PT", sb_)])
                A("dve", lambda e, off=off, sb_=sb_: e.memset(PT[64:128, sb_, :].rearrange("p (m q) -> p m q", m=2)[:, :, off:off + 64], 0.0),
                  [], [("PT", sb_)])
            else:
                A("act", lambda e, sb_=sb_: e.activation(out=PT[:, sb_, :], in_=ps[:, sb_, :], func=AF.Exp, scale=scale0), pk, [("PT", sb_)])

        def att_PV(i):
            j, kb = its[i]
            sb_ = sbs[i]
            off = 128 if kb == 2 * j + 1 else 0
            for qi in range(2):
                if qi == 0 and off:
                    continue
                bank = 4 + 2 * pvbs[j] + qi
                for m in range(2):
                    A("pe", lambda e, qi=qi, m=m, kb=kb, j=j, bank=bank, sb_=sb_, VA=VA: e.matmul(
                        ps[:, bank, m * 129:(m + 1) * 129], lhsT=PT[:, sb_, m * 256 + qi * 128: m * 256 + (qi + 1) * 128],
                        rhs=VA[:, kb, :], start=(kb == 0 and m == 0), stop=(kb == 2 * j + qi), skip_group_check=True),
                      [("PT", sb_), ("VA", b), ("VA1c", b)], [P(bank)])

        def o32_of(qi, j):
            c0 = 128 if j % 2 == 0 else 384
            return TMP[:, qi, c0:c0 + 128], ("O32", qi, j % 2)

        def att_evac1(j):
            for qi in range(2):
                bank = 4 + 2 * pvbs[j] + qi
                sm = SM[:, qi * 32 + (j % 2) * 16: qi * 32 + (j % 2) * 16 + 16]
                sk = ("SM", qi, j % 2)
                tk = ("TMP", qi)
                o32, ok = o32_of(qi, j)
                A("dve", lambda e, bank=bank, sm=sm: e.reciprocal(out=sm[:, 0:2], in_=ps[:, bank, 0:258].rearrange("p (m e) -> p m e", m=2)[:, :, 128:129].rearrange("p m e -> p (m e)")),
                  [P(bank)], [sk])
                A("dve", lambda e, sm=sm: e.tensor_tensor(out=sm[:, 2:3], in0=sm[:, 1:2], in1=NL[:], op=ALU.mult), [sk, "NL"], [sk])
                A("dve", lambda e, bank=bank, sm=sm, qi=qi: e.tensor_scalar(out=TMP[:, qi, 0:128], in0=ps[:, bank, 129:257], scalar1=sm[:, 2:3], scalar2=None, op0=ALU.mult),
                  [P(bank), sk], [tk])
                A("dve", lambda e, bank=bank, sm=sm, qi=qi, o32=o32: e.scalar_tensor_tensor(out=o32, in0=ps[:, bank, 0:128], scalar=sm[:, 0:1],
                                                                                             in1=TMP[:, qi, 0:128], op0=ALU.mult, op1=ALU.add),
                  [P(bank), sk, tk], [ok])
                A("dve", lambda e, sm=sm, qi=qi, o32=o32: e.scalar_tensor_tensor(out=TMP[:, qi, 256:384], in0=o32, scalar=1.0, in1=o32,
                                                                                  op0=ALU.mult, op1=ALU.mult, accum_out=sm[:, 3:4]),
                  [ok], [tk, sk])
                A("dve", lambda e, sm=sm: e.tensor_scalar(out=sm[:, 4:5], in0=sm[:, 3:4], scalar1=128.0 * SUBLN_EPS, scalar2=None, op0=ALU.add), [sk], [sk])
                A("pool", lambda e, sm=sm: e.tensor_tensor(out=sm[:, 5:6], in0=sm[:, 4:5], in1=mh[:, 0:1], op=ALU.pow), [sk, "mh"], [sk])

        def att_evac2(j):
            for qi in range(2):
                tb = 2 * j + qi
                sm = SM[:, qi * 32 + (j % 2) * 16: qi * 32 + (j % 2) * 16 + 16]
                sk = ("SM", qi, j % 2)
                o32, ok = o32_of(qi, j)
                A("dve", lambda e, sm=sm, o32=o32, tb=tb, h=h, G=G: e.scalar_tensor_tensor(out=RO[:, tb, h * 128:(h + 1) * 128], in0=o32, scalar=sm[:, 5:6],
                                                                                           in1=G[:, tb, :], op0=ALU.mult, op1=ALU.mult),
                  [ok, sk, ("G", b)], [("RO", tb)])

        att_S(0)
        att_S(1)
        for i in range(len(its)):
            j, kb = its[i]
            if kb == 0 and j >= 1:
                att_evac2(j - 1)
            att_exp(i)
            if i + 2 < len(its):
                att_S(i + 2)
            att_PV(i)
            if kb == 2 * j + 1:
                att_evac1(j)
        att_evac2(7)

    def phase_C(li, src_d, src_key, w_d, dst_d, dst_key):
        OT = RH[:, 6144:6144 + 2048].rearrange("p (s n) -> p s n", s=2)
        XN = RH[:, 8192:8192 + 8192].bitcast(F32)
        S.alias([("OT", 0), ("OT", 1), ("XN", 0), ("XN", 1), ("XN", 2), ("XN", 3)],
                [("QK", 0), ("QK", 1), ("VA", 0), ("VA", 1), ("G", 0), ("G", 1), ("VA1c", 0), ("VA1c", 1), "L1H",
                 ("KT", 0), ("KT", 1), ("QT", 0), ("QT", 1), ("KTr", 0), ("KTr", 1), ("QTr", 0), ("QTr", 1), "QNT", "KVNT"])
        def c_xload(tb):
            dma(Q2, XS[:, tb % 4, :], src_d[tb * 128:(tb + 1) * 128, :], r=[(src_key, tb)] if src_key else [], w=[("XS", tb % 4)])

        def c_tr(tb):
            s2 = tb % 2
            for f in range(8):
                A("pe", lambda e, f=f, tb=tb, s2=s2: e.transpose(psb(s2)[:, f * 128:(f + 1) * 128], RO[:, tb, f * 128:(f + 1) * 128], idb[:]),
                  [("RO", tb), "idb"], [P(s2)])
            A("act", lambda e, s2=s2: e.copy(out=OT[:, s2, :], in_=psb(s2)[:, :]), [P(s2)], [("OT", s2)])

        def c_mm(tb):
            s2 = tb % 2
            s4 = tb % 4
            for n in range(2):
                bank = 2 + 2 * s2 + n
                for f in range(8):
                    A("pe", lambda e, f=f, n=n, bank=bank, s2=s2: e.matmul(ps[:, bank, :], lhsT=OT[:, s2, f * 128:(f + 1) * 128], rhs=WO[:, f, n * 512:(n + 1) * 512],
                                                                          start=(f == 0), stop=(f == 7)),
                      [("OT", s2), "WO"], [P(bank)])
                A("dve", lambda e, n=n, bank=bank, s4=s4, tb=tb: e.scalar_tensor_tensor(out=XN[:, s4 * 1024 + n * 512: s4 * 1024 + (n + 1) * 512],
                                                                                       in0=XS[:, tb % 4, n * 512:(n + 1) * 512], scalar=ALPHA, in1=ps[:, bank, :],
                                                                                       op0=ALU.mult, op1=ALU.add),
                  [P(bank), ("XS", tb % 4)], [("XN", s4)])

        def c_vars(tb):
            s4 = tb % 4
            return s4, XN[:, s4 * 1024:(s4 + 1) * 1024], SM[:, 64 + s4 * 32: 64 + (s4 + 1) * 32], ("SM", 2 + s4)

        def c_A(tb):
            s4, xn, sm, sk = c_vars(tb)
            for n in range(2):
                A("dve", lambda e, xn=xn, n=n, sm=sm: e.bn_stats(out=sm[:, 8 + 6 * n: 14 + 6 * n], in_=xn[:, n * 512:(n + 1) * 512]), [("XN", s4)], [sk])
            A("dve", lambda e, sm=sm: e.bn_aggr(out=sm[:, 0:2], in_=sm[:, 8:20]), [sk], [sk])
            A("dve", lambda e, sm=sm: e.tensor_scalar(out=sm[:, 2:3], in0=sm[:, 1:2], scalar1=LN_EPS, scalar2=None, op0=ALU.add), [sk], [sk])
            A("pool", lambda e, sm=sm: e.tensor_tensor(out=sm[:, 3:4], in0=sm[:, 2:3], in1=mh[:, 0:1], op=ALU.pow), [sk, "mh"], [sk])

        def c_B(tb):
            s4, xn, sm, sk = c_vars(tb)
            A("dve", lambda e, sm=sm: e.scalar_tensor_tensor(out=sm[:, 4:5], in0=sm[:, 0:1], scalar=-1.0, in1=sm[:, 3:4], op0=ALU.mult, op1=ALU.mult),
              [sk], [sk])
            A("act", lambda e, xn=xn, sm=sm: e.activation(out=xn, in_=xn, func=AF.Identity, bias=sm[:, 4:5], scale=sm[:, 3:4]),
              [("XN", s4), sk], [("XN", s4)])

        def c_C(tb):
            s4, xn, sm, sk = c_vars(tb)
            oslot = 4 + tb % 4
            A("dve", lambda e, xn=xn: e.tensor_tensor(out=xn, in0=xn, in1=CB[:, 1, :], op=ALU.mult), [("XN", s4), "CBg"], [("XN", s4)])
            A("pool", lambda e, xn=xn, oslot=oslot: e.tensor_tensor(out=XS[:, oslot, :], in0=xn, in1=CB[:, 2, :], op=ALU.add), [("XN", s4), "CBg"], [("XS", oslot)])
            dma("sp", dst_d[tb * 128:(tb + 1) * 128, :], XS[:, oslot, :], r=[("XS", oslot)], w=[(dst_key, tb)] if dst_key else [], key=("out", oslot))

        c_xload(0)
        c_xload(1)
        c_tr(0)
        for t in range(NTB + 2):
            if t + 2 < NTB:
                c_xload(t + 2)
            if t + 1 < NTB:
                c_tr(t + 1)
            if t < NTB:
                c_mm(t)
                c_A(t)
            if 0 <= t - 1 < NTB:
                c_B(t - 1)
            if 0 <= t - 2 < NTB:
                c_C(t - 2)

    S.alias([("XS", i) for i in range(8)], [("QZ", 0), ("QZ", 1), ("QZz", 0), ("QZz", 1)])
    phase_C(0, x_d, None, w0o_d, x1_d, "x1d")

    phase_U(1, x1_d, "x1d")
    KT = [RH[:, i * 2048:(i + 1) * 2048] for i in range(2)]
    QT = [RH[:, 4096 + i * 2048: 4096 + (i + 1) * 2048] for i in range(2)]
    QNT = RH[:, 8192:16384].rearrange("p (c t) -> p c t", c=4)
    KVNT = RH[:, 16384:20480].rearrange("p (c t) -> p c t", c=2)
    S.alias(["L1H", ("KTr", 0), ("KTr", 1), ("KT", 0), ("KT", 1), ("QT", 0), ("QT", 1), ("QTr", 0), ("QTr", 1), "QNT", "KVNT"], [("OT", 0), ("OT", 1), ("XN", 0), ("XN", 1), ("XN", 2), ("XN", 3)])
    S.alias([("RO", tb) for tb in range(NTB)], [("RO", tb) for tb in range(NTB)])

    S.alias([("TMP", 0), ("TMP", 1)], [("O32", q_, p_) for q_ in range(2) for p_ in range(2)] + [("TMP", 0), ("TMP", 1)])
    S.alias([("SM", 0), ("SM", 1)], [("SM", q_, p_) for q_ in range(2) for p_ in range(2)])
    for i_ in range(2):
        kv_ = KT[i_][64:97, :].rearrange("p (k s) -> p k s", s=128)
        qv_ = QT[i_][64:97, :].rearrange("p (k s) -> p k s", s=128)
        A("pool", lambda e, kv_=kv_: e.memset(kv_[:, :, 0:64], 0.0), [], [("KTr", i_)])
        A("pool", lambda e, kv_=kv_: e.memset(kv_[:, :, 64:128], 1.0), [], [("KTr", i_)])
        A("pool", lambda e, qv_=qv_: e.memset(qv_[:, :, 0:64], -30000.0), [], [("QTr", i_)])
        A("pool", lambda e, qv_=qv_: e.memset(qv_[:, :, 64:128], 0.0), [], [("QTr", i_)])
    S.alias([("WB", 0), ("WB", 1)], ["WO"])
    ncols = [512, 288, 512, 512]
    coff = [0, 512, 800, 1312]
    def a_load(n):
        load_weights(lambda kp, n2, n=n: w1a_d[kp * 128:(kp + n2) * 128, coff[n]:coff[n] + ncols[n]].rearrange("(k p) n -> p k n", p=128),
                     8, n % 2, ("WB", n % 2), ncols=ncols[n])

    a_load(0)
    for n in range(4):
        wsl = n % 2
        if n + 1 < 4:
            a_load(n + 1)

        def a_mm(tb):
            pb = tb % 2
            for c in range(8):
                A("pe", lambda e, c=c, tb=tb, pb=pb, wsl=wsl, n=n: e.matmul(ps[:, pb, 0:ncols[n]], lhsT=UT[:, c, tb * 128:(tb + 1) * 128], rhs=WB[:, wsl, c, 0:ncols[n]],
                                                                           start=(c == 0), stop=(c == 7)),
                  [("UT", tb // 4), ("WB", wsl)], [P(pb)])

        def a_evac(tb):
            pb = tb % 2
            sm = SM[:, pb * 32:(pb + 1) * 32]
            sk = ("SM", pb)
            if n in (0, 1):
                W_ = 512 if n == 0 else 256
                gb = gqb if n == 0 else gkvb
                A("act", lambda e, pb=pb, W_=W_: e.copy(out=TMP[:, pb, 0:W_], in_=ps[:, pb, 0:W_]), [P(pb)], [("TMP", pb)])
                A("dve", lambda e, pb=pb, W_=W_, sm=sm: e.scalar_tensor_tensor(out=QKS[:, pb, 0:W_], in0=TMP[:, pb, 0:W_], scalar=1.0, in1=TMP[:, pb, 0:W_],
                                                                                op0=ALU.mult, op1=ALU.mult, accum_out=sm[:, 0:1]),
                  [("TMP", pb)], [("QKSm", pb), sk])
                A("dve", lambda e, sm=sm, W_=W_: e.tensor_scalar(out=sm[:, 1:2], in0=sm[:, 0:1], scalar1=W_ * RMS_EPS, scalar2=None, op0=ALU.add), [sk], [sk])
                A("pool", lambda e, sm=sm: e.tensor_tensor(out=sm[:, 2:3], in0=sm[:, 1:2], in1=mh[:, 0:1], op=ALU.pow), [sk, "mh"], [sk])
                if n == 1:
                    src = ps[:, pb, 256:288].rearrange("p (g d) -> p g d", g=1)
                    dst = QKS[:, pb, 256:288].rearrange("p (g d) -> p g d", g=1)
                    x1, x2, cb, sbk, t, rk = rope_ops(src, dst, CS1[:, 0, tb, :], CS1[:, 1, tb, :], 1, 16, pb)
                    A("dve", lambda e, x1=x1, cb=cb, t=t: e.tensor_tensor(out=t[0], in0=x1, in1=cb, op=ALU.mult), [P(pb), "CS"], [rk])
                    A("dve", lambda e, x2=x2, sbk=sbk, t=t: e.tensor_tensor(out=t[1], in0=x2, in1=sbk, op=ALU.mult), [P(pb), "CS"], [rk])
                    A("dve", lambda e, x2=x2, cb=cb, t=t: e.tensor_tensor(out=t[2], in0=x2, in1=cb, op=ALU.mult), [P(pb), "CS"], [rk])
                    A("dve", lambda e, x1=x1, sbk=sbk, t=t: e.tensor_tensor(out=t[3], in0=x1, in1=sbk, op=ALU.mult), [P(pb), "CS"], [rk])
                    A("dve", lambda e, dst=dst, t=t: e.tensor_tensor(out=dst[:, :, 0:16], in0=t[0], in1=t[1], op=ALU.subtract), [rk], [("QKSr", pb)])
                    A("dve", lambda e, dst=dst, t=t: e.tensor_tensor(out=dst[:, :, 16:32], in0=t[2], in1=t[3], op=ALU.add), [rk], [("QKSr", pb)])
            else:
                gc = (n - 2) * 512
                A("act", lambda e, pb=pb, tb=tb, gc=gc: e.activation(out=RO[:, tb, gc:gc + 512], in_=ps[:, pb, :], func=AF.Silu), [P(pb)], [("RO", tb)])

        def a_norm(tb):
            if n not in (0, 1):
                return
            pb = tb % 2
            sm = SM[:, pb * 32:(pb + 1) * 32]
            sk = ("SM", pb)
            W_ = 512 if n == 0 else 256
            gb = gqb if n == 0 else gkvb
            A("dve", lambda e, pb=pb, W_=W_, sm=sm, gb=gb: e.scalar_tensor_tensor(out=QKS[:, pb, 0:W_], in0=TMP[:, pb, 0:W_], scalar=sm[:, 2:3], in1=gb[:, 0:W_],
                                                                                   op0=ALU.mult, op1=ALU.mult),
              [("TMP", pb), sk, "gqb", "gkvb"], [("QKSm", pb)])

        def a_tr(tb):
            if n not in (0, 1):
                return
            pb = tb % 2
            tbk = 2 + pb
            W_ = 512 if n == 0 else 256
            nt = W_ // 128
            for a in range(nt):
                A("pe", lambda e, a=a, pb=pb, tbk=tbk: e.transpose(psb(tbk)[:, a * 128:(a + 1) * 128], QKS[:, pb, a * 128:(a + 1) * 128], idb[:]),
                  [("QKSm", pb), "idb"], [P(tbk)])
            if n == 1:
                A("pe", lambda e, pb=pb, tbk=tbk: e.transpose(psb(tbk)[0:32, 256:384], QKS[:, pb, 256:288], idb[:]), [("QKSr", pb), "idb"], [P(tbk)])
            dstT = QNT if n == 0 else KVNT
            A("dve", lambda e, dstT=dstT, tb=tb, tbk=tbk, nt=nt: e.tensor_copy(out=dstT[:, :, tb * 128:(tb + 1) * 128],
                                                                               in_=psb(tbk)[:, 0:nt * 128].rearrange("p (a t) -> p a t", a=nt)),
              [P(tbk)], ["QNT" if n == 0 else "KVNT"])
            if n == 1:
                for i in range(2):
                    A("dve", lambda e, i=i, tb=tb, tbk=tbk: e.tensor_copy(out=KT[i][64:96, tb * 128:(tb + 1) * 128], in_=psb(tbk)[0:32, 256:384]),
                      [P(tbk)], [("KTr", i)])

        for t in range(NTB + 2):
            if t < NTB:
                a_mm(t)
            if 0 <= t - 2 < NTB:
                a_tr(t - 2)
            if t < NTB:
                a_evac(t)
            if 0 <= t - 1 < NTB:
                a_norm(t - 1)

    VA1 = RU[:, 0:16640].rearrange("p (k h e) -> p k h e", k=16, h=16)
    S.alias(["VA1", "VA1c"], [("UT", g) for g in range(4)])
    S.alias(["QRT"], ["CB", "CBg"])
    A("pool", lambda e: e.memset(VA1[:, :, :, 64:65], 1.0), [], ["VA1c"])
    for nh in range(2):
        wsl = nh % 2
        load_weights(lambda kp, n2, nh=nh: wukvv_d[kp * 128:(kp + n2) * 128, nh * 512:(nh + 1) * 512].rearrange("(k p) n -> p k n", p=128), 2, wsl, ("WB", wsl))
        for tb in range(NTB):
            pb = tb % 2
            for c in range(2):
                A("pe", lambda e, c=c, tb=tb, pb=pb, wsl=wsl: e.matmul(ps[:, pb, :], lhsT=KVNT[:, c, tb * 128:(tb + 1) * 128], rhs=WB[:, wsl, c, :],
                                                                      start=(c == 0), stop=(c == 1)),
                  ["KVNT", ("WB", wsl)], [P(pb)])
            A("act", lambda e, tb=tb, pb=pb, nh=nh: e.copy(out=VA1[:, tb, nh * 8:(nh + 1) * 8, 0:64], in_=ps[:, pb, :].rearrange("p (h e) -> p h e", h=8)),
              [P(pb)], ["VA1"])
    load_weights(lambda kp, n2: wuqr_d[kp * 128:(kp + n2) * 128, :].rearrange("(k p) n -> p k n", p=128), 4, 0, ("WB", 0))
    def r_mm(tb):
        pb = tb % 2
        for c in range(4):
            A("pe", lambda e, c=c, tb=tb, pb=pb: e.matmul(ps[:, pb, :], lhsT=QNT[:, c, tb * 128:(tb + 1) * 128], rhs=WB[:, 0, c, :], start=(c == 0), stop=(c == 3)),
              ["QNT", ("WB", 0)], [P(pb)])
        src = ps[:, pb, :].rearrange("p (g d) -> p g d", g=16)
        dst = QKS[:, pb, :].rearrange("p (g d) -> p g d", g=16)
        x1, x2, cb, sbk, t, rk = rope_ops(src, dst, CS1[:, 0, tb, :], CS1[:, 1, tb, :], 16, 16, pb)
        if tb == 0:
            S.alias(["angT"], ["ang0", "ang1", "ang2", "ang3", ("TMP", 0), ("TMP", 1)])
        tt = [ang[:, i, 0:256].rearrange("p (a b) -> p a b", a=16) for i in range(4)]
        rk = "angT"
        A("dve", lambda e, x1=x1, cb=cb, tt=tt: e.tensor_tensor(out=tt[0], in0=x1, in1=cb, op=ALU.mult), [P(pb), "CS"], [rk])
        A("dve", lambda e, x2=x2, sbk=sbk, tt=tt: e.tensor_tensor(out=tt[1], in0=x2, in1=sbk, op=ALU.mult), [P(pb), "CS"], [rk])
        A("dve", lambda e, x2=x2, cb=cb, tt=tt: e.tensor_tensor(out=tt[2], in0=x2, in1=cb, op=ALU.mult), [P(pb), "CS"], [rk])
        A("dve", lambda e, x1=x1, sbk=sbk, tt=tt: e.tensor_tensor(out=tt[3], in0=x1, in1=sbk, op=ALU.mult), [P(pb), "CS"], [rk])
        A("dve", lambda e, dst=dst, tt=tt: e.tensor_tensor(out=dst[:, :, 0:16], in0=tt[0], in1=tt[1], op=ALU.subtract), [rk], [("QKSm", pb)])
        A("dve", lambda e, dst=dst, tt=tt: e.tensor_tensor(out=dst[:, :, 16:32], in0=tt[2], in1=tt[3], op=ALU.add), [rk], [("QKSm", pb)])

    def r_tr(tb):
        pb = tb % 2
        tbk = 2 + pb
        for a in range(4):
            A("pe", lambda e, a=a, pb=pb, tbk=tbk: e.transpose(psb(tbk)[:, a * 128:(a + 1) * 128], QKS[:, pb, a * 128:(a + 1) * 128], idb[:]),
              [("QKSm", pb), "idb"], [P(tbk)])
        A("dve", lambda e, tb=tb, tbk=tbk: e.tensor_copy(out=QRT[:, :, tb * 128:(tb + 1) * 128], in_=psb(tbk)[:, 0:512].rearrange("p (a t) -> p a t", a=4)),
          [P(tbk)], ["QRT"])

    for tb in range(NTB):
        r_mm(tb)
        if tb >= 1:
            r_tr(tb - 1)
    r_tr(NTB - 1)

    scale1 = 96 ** -0.5
    WH = WB[:, 1, :, :].rearrange("p c n -> p (c n)")
    S.alias([("WH", 0), ("WH", 1)], [("WB", 1)])
    def l1_proj(h):
        b = h % 2
        wq = WH[:, b * 512: b * 512 + 256].rearrange("p (c n) -> p c n", c=4)
        wk = WH[:, b * 512 + 256: b * 512 + 384].rearrange("p (c n) -> p c n", c=2)
        ws = load_weights.ctr % 2
        load_weights.ctr += 1
        dma("sp", WS[:, ws, 0, 0:256].rearrange("p (c n) -> p c n", c=4), wuqn_d[h].rearrange("(c p) n -> p c n", p=128), w=[("WS", ws)])
        dma("sp", WS[:, ws, 1, 0:128].rearrange("p (c n) -> p c n", c=2), wukvn_d[h].rearrange("(c p) n -> p c n", p=128), w=[("WS", ws)])
        A("pool", lambda e, ws=ws, wq=wq: e.tensor_copy(out=wq, in_=WS[:, ws, 0, 0:256].rearrange("p (c n) -> p c n", c=4)), [("WS", ws)], [("WH", b)])
        A("pool", lambda e, ws=ws, wk=wk: e.tensor_copy(out=wk, in_=WS[:, ws, 1, 0:128].rearrange("p (c n) -> p c n", c=2)), [("WS", ws)], [("WH", b)])
        for tt_ in range(4):
            pk_ = 4
            for c in range(2):
                A("pe", lambda e, c=c, tt_=tt_, pk_=pk_, wk=wk: e.matmul(ps[0:64, pk_, :], lhsT=wk[:, c, :], rhs=KVNT[:, c, tt_ * 512:(tt_ + 1) * 512], start=(c == 0), stop=(c == 1)),
                  [("WH", b), "KVNT"], [P(pk_)])
            A("dve", lambda e, tt_=tt_, pk_=pk_, b=b: e.tensor_copy(out=KT[b][0:64, tt_ * 512:(tt_ + 1) * 512], in_=ps[0:64, pk_, :]), [P(pk_)], [("KT", b)])
            pq_ = (5, 7)[tt_ % 2]
            for c in range(4):
                A("pe", lambda e, c=c, tt_=tt_, pq_=pq_, wq=wq: e.matmul(ps[0:64, pq_, :], lhsT=wq[:, c, :], rhs=QNT[:, c, tt_ * 512:(tt_ + 1) * 512], start=(c == 0), stop=(c == 3)),
                  [("WH", b), "QNT"], [P(pq_)])
            A("dve", lambda e, tt_=tt_, pq_=pq_, b=b: e.tensor_copy(out=QT[b][0:64, tt_ * 512:(tt_ + 1) * 512], in_=ps[0:64, pq_, :]), [P(pq_)], [("QT", b)])
        A("pool", lambda e, b=b, h=h: e.tensor_copy(out=QT[b][64:96, :], in_=QRT[(h % 4) * 32:(h % 4) * 32 + 32, h // 4, :]), ["QRT"], [("QTr", b)])

    l1_proj(0)
    for h in range(16):
        b = h % 2
        if h + 1 < 16:
            l1_proj(h + 1)
        if h + 1 == 15:
            phase_C_pre(1, w1o_d)
        its = [(j, kb) for j in range(4) for kb in range(4 * j + 4)]
        sbs = []
        pvbs = {}
        for (j, kb) in its:
            sbs.append((0, 1, 6)[sctr[0] % 3])
            sctr[0] += 1
            if j not in pvbs:
                pvbs[j] = 2 + pvctr[0] % 2
                pvctr[0] += 1

        def a1_S(i):
            j, kb = its[i]
            sb_ = sbs[i]
            d = kb - 4 * j
            off = 128 * d if d > 0 else 0
            keys_ = [("KT", b), ("KTr", b), ("QT", b), ("QTr", b)]
            if d >= 0:
                A("pe", lambda e, kb=kb, j=j, off=off, sb_=sb_, b=b: e.matmul(ps[:, sb_, off:off + 128], lhsT=KT[b][0:97, kb * 128:(kb + 1) * 128],
                                                                             rhs=QT[b][0:97, j * 512 + off: j * 512 + off + 128], start=True, stop=True), keys_, [P(sb_)])
                if off + 128 < 512:
                    A("pe", lambda e, kb=kb, j=j, off=off, sb_=sb_, b=b: e.matmul(ps[:, sb_, off + 128:512], lhsT=KT[b][0:96, kb * 128:(kb + 1) * 128],
                                                                                 rhs=QT[b][0:96, j * 512 + off + 128:(j + 1) * 512], start=True, stop=True), keys_, [P(sb_)])
            else:
                A("pe", lambda e, kb=kb, j=j, sb_=sb_, b=b: e.matmul(ps[:, sb_, :], lhsT=KT[b][0:96, kb * 128:(kb + 1) * 128],
                                                                    rhs=QT[b][0:96, j * 512:(j + 1) * 512], start=True, stop=True), keys_, [P(sb_)])

        def a1_exp(i):
            j, kb = its[i]
            sb_ = sbs[i]
            pt_ = (0, 1, 6).index(sb_)
            d = kb - 4 * j
            off = 128 * d if d > 0 else 0
            A("act", lambda e, sb_=sb_, pt_=pt_, off=off: e.activation(out=PT[:, pt_, off:512], in_=ps[:, sb_, off:512], func=AF.Exp, scale=scale1),
              [P(sb_)], [("PT", pt_)])

        def a1_PV(i):
            j, kb = its[i]
            sb_ = sbs[i]
            d = kb - 4 * j
            pvb = pvbs[j]
            pt_ = (0, 1, 6).index(sb_)
            for qi in range(max(d, 0), 4):
                A("pe", lambda e, qi=qi, kb=kb, pvb=pvb, pt_=pt_, h=h, j=j: e.matmul(
                    ps[:, pvb, qi * 65:(qi + 1) * 65], lhsT=PT[:, pt_, qi * 128:(qi + 1) * 128], rhs=VA1[:, kb, h, :],
                    start=(kb == 0 and qi == 0), stop=(kb == 4 * j + qi), skip_group_check=True),
                  [("PT", pt_), "VA1", "VA1c"], [P(pvb)])

        def a1_evac(j):
            pvb = pvbs[j]
            sm = SM[:, (j % 2) * 32:(j % 2 + 1) * 32]
            sk = ("SM", j % 2)
            A("dve", lambda e, pvb=pvb, sm=sm: e.reciprocal(out=sm[:, 0:4], in_=ps[:, pvb, 0:260].rearrange("p (q e) -> p q e", q=4)[:, :, 64:65].rearrange("p q e -> p (q e)")), [P(pvb)], [sk])
            for qi in range(4):
                tb = 4 * j + qi
                A("dve", lambda e, pvb=pvb, sm=sm, qi=qi, tb=tb, h=h: e.scalar_tensor_tensor(
                    out=RO[:, tb, h * 64:(h + 1) * 64], in0=ps[:, pvb, qi * 65: qi * 65 + 64], scalar=sm[:, qi:qi + 1],
                    in1=RO[:, tb, h * 64:(h + 1) * 64], op0=ALU.mult, op1=ALU.mult),
                  [P(pvb), sk, ("RO", tb)], [("RO", tb)])

        a1_S(0)
        a1_S(1)
        for i in range(len(its)):
            a1_exp(i)
            if i + 2 < len(its):
                a1_S(i + 2)
            a1_PV(i)
            j, kb = its[i]
            if kb == 4 * j + 3:
                a1_evac(j)

    phase_C(1, x1_d, "x1d", w1o_d, y_d, None)
    S.emit(final_wait_ops=[o for o in S.ops if o.is_dma])
    return nc, S


_CACHE = {}


def _host_inputs(x, c, positions, ada_w, ada_b, ln_g, ln_b, a_w_in, a_lambda_q1, a_lambda_k1, a_lambda_q2,
                 a_lambda_k2, a_subln_g, a_w_out, b_w_in, b_q_norm_g, b_w_uq, b_kv_norm_g, b_w_ukv, b_w_out):
    f = lambda a: np.ascontiguousarray(np.asarray(a, dtype=np.float32))
    a_w_in = f(a_w_in)[0]
    w0h = np.stack([np.concatenate([a_w_in[:, s * 1024 + h * 128: s * 1024 + (h + 1) * 128] for s in range(4)], axis=1)
                    for h in range(8)], axis=0)
    w_uq = f(b_w_uq)[0].reshape(512, 16, 96)
    w_ukv = f(b_w_ukv)[0].reshape(256, 16, 128)
    shared = {
        "ada_w": f(ada_w), "ada_b": f(ada_b).reshape(1, -1), "ln_g": f(ln_g), "ln_b": f(ln_b),
        "w0h": np.ascontiguousarray(w0h),
        "lamrow": np.concatenate([f(a_lambda_q1)[0], f(a_lambda_k1)[0], f(a_lambda_q2)[0], f(a_lambda_k2)[0]]).reshape(1, 256),
        "subg": f(a_subln_g).reshape(1, 128), "w0o": f(a_w_out)[0], "w1a": f(b_w_in)[0],
        "gq": f(b_q_norm_g).reshape(1, 512), "gkv": f(b_kv_norm_g).reshape(1, 256),
        "wuqn": np.ascontiguousarray(w_uq[:, :, 0:64].transpose(1, 0, 2)),
        "wuqr": np.ascontiguousarray(w_uq[:, :, 64:96].reshape(512, 512)),
        "wukvn": np.ascontiguousarray(w_ukv[:, :, 0:64].transpose(1, 0, 2)),
        "wukvv": np.ascontiguousarray(w_ukv[:, :, 64:128].reshape(256, 1024)),
        "w1o": f(b_w_out)[0],
        "idf": np.eye(128, dtype=np.float32),
        "invf0": np.tile((np.float32(500000.0) ** (-np.arange(0, 16, 2, dtype=np.float32) / np.float32(16))).astype(np.float32)[None, :], (128, 1)),
        "invf1": np.tile((np.float32(500000.0) ** (-np.arange(0, 32, 2, dtype=np.float32) / np.float32(32))).astype(np.float32)[None, :], (128, 1)),
        "maskb": np.concatenate([np.zeros(64, np.float32), np.full(64, -30000.0, np.float32)]).reshape(128, 1),
    }
    x = f(x)
    c = f(c)
    positions = np.asarray(positions, dtype=np.int32)
    maps = []
    for b in range(8):
        m = dict(shared)
        m["x"] = np.ascontiguousarray(x[b])
        m["cT"] = np.ascontiguousarray(c[b].reshape(8, 128).T)
        m["pos"] = np.ascontiguousarray(positions[b].reshape(16, 128).T)
        maps.append(m)
    return maps


def kernel(**inputs):
    if "nc" not in _CACHE:
        _CACHE["nc"] = build_program()[0]
    nc = _CACHE["nc"]
    maps = _host_inputs(**inputs)
    res = run_bass_kernel_spmd(nc, maps, core_ids=list(range(8)))
    return np.stack([np.asarray(r["y"], dtype=np.float32) for r in res.results], axis=0)
```
